# Optimizing a Trainium2 kernel written in Bass

```python
import math
import jax, jax.numpy as jnp
from jax import lax
import numpy as np

D_MODEL = 1024
BATCH = 32
SEQ = 256
DEPTH = 2
DEC_BATCH = 2
DEC_SEQ = 2048
PAST_LEN = 512

GRID_W = 64
ROPE_THETA = 10000.0
NORM_EPS = 1e-6
Q_BLOCK = 128

D_MIX = D_MODEL
BRANCH = D_MIX // 4

HEAD_DIM = 64
GQA_HEADS = BRANCH // HEAD_DIM
GQA_KV_HEADS = GQA_HEADS // 2

GLA_HEADS = 4
GLA_DV = BRANCH // GLA_HEADS
GLA_DK = GLA_DV // 2
GLA_LR = 16
GLA_TAU = 16.0
GLA_CHUNK = 64

MLA_HEADS = 4
MLA_V = BRANCH // MLA_HEADS
MLA_NOPE = 64
MLA_ROPE = 32
MLA_Q_RANK = 3 * D_MODEL // 16
MLA_KV_RANK = D_MODEL // 8

SSD_HEADS = 4
SSD_HEAD_DIM = BRANCH // SSD_HEADS
SSD_GROUPS = 2
SSD_STATE = 64
SSD_CONV = 5
SSD_CHUNK = 64
SSD_CONV_CH = BRANCH + 2 * SSD_GROUPS * SSD_STATE

PROJ_SIZES = (
    GQA_HEADS * HEAD_DIM, GQA_KV_HEADS * HEAD_DIM, GQA_KV_HEADS * HEAD_DIM, BRANCH,
    GLA_HEADS * GLA_DK, GLA_HEADS * GLA_DK, GLA_HEADS * GLA_DV, 2 * GLA_LR, BRANCH,
    MLA_Q_RANK, MLA_KV_RANK, MLA_ROPE, BRANCH,
    BRANCH, SSD_CONV_CH, 2 * SSD_HEADS,
)
PROJ_SPLITS = tuple(sum(PROJ_SIZES[:i + 1]) for i in range(len(PROJ_SIZES) - 1))
PROJ_DIM = sum(PROJ_SIZES)

kernel_name = "hybrid_prefix_diffusion_trunk_step"

F32 = jnp.float32


def rms_norm(x, g):
    xf = x.astype(F32)
    y = xf * lax.rsqrt(jnp.mean(xf * xf, axis=-1, keepdims=True) + NORM_EPS)
    return (y * g.astype(F32)).astype(x.dtype)


def axial_rope_angles(n_tok, rot_dim):
    rows = n_tok // GRID_W
    row = jnp.repeat(jnp.arange(rows), GRID_W).astype(F32)
    col = jnp.tile(jnp.arange(GRID_W), rows).astype(F32)
    n_freq = rot_dim // 4
    inv = ROPE_THETA ** (-jnp.arange(n_freq, dtype=F32) / n_freq)
    ang = jnp.stack([row[:, None] * inv, col[:, None] * inv], axis=1)
    return jnp.cos(ang), jnp.sin(ang)


def apply_axial_rope(x, cos, sin):
    B, N, H, R = x.shape
    nf = R // 4
    xr = x.astype(F32).reshape(B, N, H, 2, 2, nf)
    x1, x2 = xr[..., 0, :], xr[..., 1, :]
    cs, sn = cos[None, :, None], sin[None, :, None]
    out = jnp.stack([x1 * cs - x2 * sn, x2 * cs + x1 * sn], axis=-2)
    return out.reshape(B, N, H, R).astype(x.dtype)


def block_attention(q, k, v, scale):
    B, Nq = q.shape[0], q.shape[1]
    nb = Nq // Q_BLOCK
    qb = q.reshape(B, nb, Q_BLOCK, *q.shape[2:]).swapaxes(0, 1)

    def one_block(qblk):
        s = jnp.einsum('bqhgd,bkhd->bhgqk', qblk, k, preferred_element_type=F32) * scale
        p = jax.nn.softmax(s, axis=-1)
        return jnp.einsum('bhgqk,bkhd->bqhgd', p.astype(v.dtype), v)

    o = lax.map(one_block, qb)
    return o.swapaxes(0, 1).reshape(B, Nq, q.shape[2], q.shape[3], v.shape[-1])


def centred_dwconv(x, w, b):
    out = lax.conv_general_dilated(
        x, w[:, None, :].astype(x.dtype), window_strides=(1,),
        padding=[(SSD_CONV // 2, SSD_CONV // 2)],
        dimension_numbers=('NWC', 'WIO', 'NWC'), feature_group_count=x.shape[-1])
    return out + b.astype(x.dtype)


def gla_chunked(q, k, v, logg, s0):
    B, N, H, dk = q.shape
    dv = v.shape[-1]
    C = GLA_CHUNK
    nc = N // C
    qc = q.astype(F32).reshape(B, nc, C, H, dk)
    kc = k.astype(F32).reshape(B, nc, C, H, dk)
    vc = v.astype(F32).reshape(B, nc, C, H, dv)
    b = jnp.cumsum(logg.astype(F32).reshape(B, nc, C, H, dk), axis=2)
    b_last = b[:, :, -1:]
    q_dec = qc * jnp.exp(b)
    k_in = kc * jnp.exp(-b)
    k_out = kc * jnp.exp(b_last - b)
    mask = jnp.tril(jnp.ones((C, C), bool))
    att = jnp.einsum('bnihd,bnjhd->bnhij', q_dec, k_in)
    att = jnp.where(mask, att, 0.0)
    o_intra = jnp.einsum('bnhij,bnjhe->bnihe', att, vc)
    chunk_upd = jnp.einsum('bnjhd,bnjhe->bnhde', k_out, vc)
    chunk_dec = jnp.exp(b_last[:, :, 0])

    def step(s, inp):
        qd, dec, upd = inp
        o = jnp.einsum('bihd,bhde->bihe', qd, s)
        return dec[..., None] * s + upd, o

    s_fin, o_inter = lax.scan(step, s0.astype(F32),
                              (q_dec.swapaxes(0, 1), chunk_dec.swapaxes(0, 1), chunk_upd.swapaxes(0, 1)))
    o = o_intra + o_inter.swapaxes(0, 1)
    return o.reshape(B, N, H, dv), s_fin


def ssd_chunked(x, dt, a, bm, cm, h0):
    B, N, H, P = x.shape
    G, S = bm.shape[2], bm.shape[3]
    R = H // G
    C = SSD_CHUNK
    nc = N // C
    xc = (x.astype(F32) * dt[..., None]).reshape(B, nc, C, G, R, P)
    cum = jnp.cumsum(a.reshape(B, nc, C, G, R), axis=2)
    bc = bm.astype(F32).reshape(B, nc, C, G, S)
    cc = cm.astype(F32).reshape(B, nc, C, G, S)
    causal = jnp.tril(jnp.ones((C, C), bool))[:, :, None, None]
    seg = cum[:, :, :, None] - cum[:, :, None, :]
    decay_ij = jnp.exp(jnp.where(causal, seg, -jnp.inf))
    scores = jnp.einsum('bnigs,bnjgs->bnijg', cc, bc)
    y_intra = jnp.einsum('bnijg,bnijgr,bnjgrp->bnigrp', scores, decay_ij, xc)
    decay_out = jnp.exp(cum[:, :, -1:] - cum)
    chunk_state = jnp.einsum('bnjgs,bnjgr,bnjgrp->bngrps', bc, decay_out, xc)
    chunk_decay = jnp.exp(cum[:, :, -1])

    def step(h, inp):
        c_n, cum_n, dec_n, st_n = inp
        y = jnp.einsum('bigs,bgrps->bigrp', c_n, h) * jnp.exp(cum_n)[..., None]
        return dec_n[..., None, None] * h + st_n, y

    h_fin, y_inter = lax.scan(step, h0.astype(F32).reshape(B, G, R, P, S),
                              (cc.swapaxes(0, 1), cum.swapaxes(0, 1), chunk_decay.swapaxes(0, 1), chunk_state.swapaxes(0, 1)))
    y = y_intra + y_inter.swapaxes(0, 1)
    return y.reshape(B, N, H, P), h_fin.reshape(B, H, P, S)


def flip(t):
    return jnp.flip(t, axis=1)


def trunk_layer(x, mod, p, cache):
    is_ctx = cache is None
    B, N, _ = x.shape
    dt_x = x.dtype
    shift, scale, gate = jnp.split(mod, 3, axis=-1)
    h = rms_norm(x, p['norm_pre']) * (1 + scale) + shift
    u = h @ p['w_in']
    (gq, gk, gv, ggate, lq, lk, lv, lglr, lgate, mcq, mckv, mkr, mgate, sz, sxbc, sdt) = jnp.split(u, PROJ_SPLITS, axis=-1)
    if not is_ctx:
        k_gqa_c, v_gqa_c, ckv_c, kr_c, s_gla_c, s_ssd_c = cache
        cos_h, sin_h = axial_rope_angles(N, HEAD_DIM)
        cos_r, sin_r = axial_rope_angles(N, MLA_ROPE)

    qa = rms_norm(gq.reshape(B, N, GQA_HEADS, HEAD_DIM), p['gqa_q_norm'])
    ka = rms_norm(gk.reshape(B, N, GQA_KV_HEADS, HEAD_DIM), p['gqa_k_norm'])
    va = gv.reshape(B, N, GQA_KV_HEADS, HEAD_DIM)
    if is_ctx:
        qa_use, ka_all, va_all = qa, ka, va
    else:
        qa_use = apply_axial_rope(qa, cos_h, sin_h)
        ka_all = jnp.concatenate([k_gqa_c.astype(dt_x), apply_axial_rope(ka, cos_h, sin_h)], axis=1)
        va_all = jnp.concatenate([v_gqa_c.astype(dt_x), va], axis=1)
    o_a = block_attention(qa_use.reshape(B, N, GQA_KV_HEADS, GQA_HEADS // GQA_KV_HEADS, HEAD_DIM),
                          ka_all, va_all, HEAD_DIM ** -0.5).reshape(B, N, BRANCH)
    o_a = o_a * jax.nn.silu(ggate)

    qb = lq.reshape(B, N, GLA_HEADS, GLA_DK) * (GLA_DK ** -0.5)
    kb = lk.reshape(B, N, GLA_HEADS, GLA_DK)
    vb = lv.reshape(B, N, GLA_HEADS, GLA_DV)
    glr = lglr.reshape(B, N, 2, GLA_LR).astype(F32)
    logg = jax.nn.log_sigmoid(jnp.einsum('bndr,dre->bnde', glr, p['gla_w_gate'].astype(F32))
                              + p['gla_b_gate'].astype(F32)) / GLA_TAU
    logg = logg.reshape(B, N, 2, GLA_HEADS, GLA_DK)
    s_gla0 = jnp.zeros((B, 2, GLA_HEADS, GLA_DK, GLA_DV), F32) if is_ctx else s_gla_c
    o_f, s_f = gla_chunked(qb, kb, vb, logg[:, :, 0], s_gla0[:, 0])
    o_bw, s_bw = gla_chunked(flip(qb), flip(kb), flip(vb), flip(logg[:, :, 1]), s_gla0[:, 1])
    o_gla = rms_norm(o_f + flip(o_bw), p['gla_norm']).reshape(B, N, BRANCH).astype(dt_x)
    o_b = o_gla * jax.nn.silu(lgate)

    cq = rms_norm(mcq, p['mla_q_norm'])
    qm = (cq @ p['mla_w_uq']).reshape(B, N, MLA_HEADS, MLA_NOPE + MLA_ROPE)
    q_nope, q_rope = qm[..., :MLA_NOPE], qm[..., MLA_NOPE:]
    ckv = rms_norm(mckv, p['mla_kv_norm'])
    if is_ctx:
        ckv_all, kr_all = ckv, mkr
    else:
        q_rope = apply_axial_rope(q_rope, cos_r, sin_r)
        kr_lat = apply_axial_rope(mkr[:, :, None, :], cos_r, sin_r)[:, :, 0]
        ckv_all = jnp.concatenate([ckv_c.astype(dt_x), ckv], axis=1)
        kr_all = jnp.concatenate([kr_c.astype(dt_x), kr_lat], axis=1)
    nk = ckv_all.shape[1]
    kvm = (ckv_all @ p['mla_w_ukv']).reshape(B, nk, MLA_HEADS, MLA_NOPE + MLA_V)
    k_nope, v_m = kvm[..., :MLA_NOPE], kvm[..., MLA_NOPE:]
    k_m = jnp.concatenate([k_nope, jnp.broadcast_to(kr_all[:, :, None, :], (B, nk, MLA_HEADS, MLA_ROPE))], axis=-1)
    q_m = jnp.concatenate([q_nope, q_rope], axis=-1)[:, :, :, None, :]
    o_c = block_attention(q_m, k_m, v_m, (MLA_NOPE + MLA_ROPE) ** -0.5).reshape(B, N, BRANCH)
    o_c = o_c * jax.nn.silu(mgate)

    xbc = jax.nn.silu(centred_dwconv(sxbc, p['ssd_conv_w'], p['ssd_conv_b']))
    xs, bs, cs = jnp.split(xbc, [BRANCH, BRANCH + SSD_GROUPS * SSD_STATE], axis=-1)
    xs = xs.reshape(B, N, SSD_HEADS, SSD_HEAD_DIM)
    bs = bs.reshape(B, N, SSD_GROUPS, SSD_STATE)
    cs = cs.reshape(B, N, SSD_GROUPS, SSD_STATE)
    dt = jax.nn.softplus(sdt.reshape(B, N, 2, SSD_HEADS).astype(F32) + p['ssd_dt_bias'].astype(F32))
    a = dt * (-jnp.exp(p['ssd_a_log'].astype(F32)))
    h_ssd0 = jnp.zeros((B, 2, SSD_HEADS, SSD_HEAD_DIM, SSD_STATE), F32) if is_ctx else s_ssd_c
    y_f, h_f = ssd_chunked(xs, dt[:, :, 0], a[:, :, 0], bs, cs, h_ssd0[:, 0])
    y_bw, h_bw = ssd_chunked(flip(xs), flip(dt[:, :, 1]), flip(a[:, :, 1]), flip(bs), flip(cs), h_ssd0[:, 1])
    y_ssd = y_f + flip(y_bw) + p['ssd_d'].astype(F32)[:, None] * xs.astype(F32)
    o_d = rms_norm(y_ssd.reshape(B, N, BRANCH) * jax.nn.silu(sz.astype(F32)), p['ssd_norm']).astype(dt_x)

    o = jnp.concatenate([o_a, o_b, o_c, o_d], axis=-1) @ p['w_out']
    y = x + gate * rms_norm(o, p['norm_post'])
    if is_ctx:
        gla_state = jnp.stack([s_f, s_bw], axis=1).astype(dt_x)
        ssd_state = jnp.stack([h_f, h_bw], axis=1).astype(dt_x)
        return y, (ka, va, ckv, mkr, gla_state, ssd_state)
    return y


def setup_inputs(seed: int = 0) -> dict:
    key = jax.random.key(seed)
    ks = jax.random.split(key, 32)

    def nrm(k, shape, s):
        return jax.random.normal(k, shape, F32) * s

    def gain(k, shape):
        return 1.0 + 0.05 * jax.random.normal(k, shape, F32)

    dt0 = jnp.exp(jax.random.uniform(ks[28], (DEPTH, 2, SSD_HEADS), F32, math.log(1e-3), math.log(1e-1)))
    return {
        "x_prompt": nrm(ks[0], (BATCH, SEQ, D_MODEL), 1.0),
        "x_sample": nrm(ks[1], (DEC_BATCH, DEC_SEQ, D_MODEL), 1.0),
        "c": nrm(ks[2], (DEC_BATCH, D_MODEL), 1.0),
        "cache_gqa_k": nrm(ks[3], (DEC_BATCH, DEPTH, PAST_LEN, GQA_KV_HEADS, HEAD_DIM), 1.0),
        "cache_gqa_v": nrm(ks[4], (DEC_BATCH, DEPTH, PAST_LEN, GQA_KV_HEADS, HEAD_DIM), 1.0),
        "cache_mla_ckv": nrm(ks[5], (DEC_BATCH, DEPTH, PAST_LEN, MLA_KV_RANK), 1.0),
        "cache_mla_krope": nrm(ks[6], (DEC_BATCH, DEPTH, PAST_LEN, MLA_ROPE), 1.0),
        "state_gla": nrm(ks[7], (DEC_BATCH, DEPTH, 2, GLA_HEADS, GLA_DK, GLA_DV), 1.0),
        "state_ssd": nrm(ks[8], (DEC_BATCH, DEPTH, 2, SSD_HEADS, SSD_HEAD_DIM, SSD_STATE), 0.5),
        "c_ctx": nrm(ks[9], (D_MODEL,), 1.0),
        "w_ada": nrm(ks[10], (DEPTH, D_MODEL, 3 * D_MODEL), 0.5 * D_MODEL ** -0.5),
        "b_ada": nrm(ks[11], (DEPTH, 3 * D_MODEL), 0.02),
        "norm_pre": gain(ks[12], (DEPTH, D_MODEL)),
        "norm_post": gain(ks[13], (DEPTH, D_MODEL)),
        "w_in": nrm(ks[14], (DEPTH, D_MODEL, PROJ_DIM), D_MODEL ** -0.5),
        "w_out": nrm(ks[15], (DEPTH, D_MIX, D_MODEL), D_MIX ** -0.5),
        "gqa_q_norm": gain(ks[16], (DEPTH, HEAD_DIM)),
        "gqa_k_norm": gain(ks[17], (DEPTH, HEAD_DIM)),
        "gla_w_gate": nrm(ks[18], (DEPTH, 2, GLA_LR, GLA_HEADS * GLA_DK), GLA_LR ** -0.5),
        "gla_b_gate": nrm(ks[19], (DEPTH, 2, GLA_HEADS * GLA_DK), 0.1),
        "gla_norm": gain(ks[20], (DEPTH, GLA_DV)),
        "mla_q_norm": gain(ks[21], (DEPTH, MLA_Q_RANK)),
        "mla_kv_norm": gain(ks[22], (DEPTH, MLA_KV_RANK)),
        "mla_w_uq": nrm(ks[23], (DEPTH, MLA_Q_RANK, MLA_HEADS * (MLA_NOPE + MLA_ROPE)), MLA_Q_RANK ** -0.5),
        "mla_w_ukv": nrm(ks[24], (DEPTH, MLA_KV_RANK, MLA_HEADS * (MLA_NOPE + MLA_V)), MLA_KV_RANK ** -0.5),
        "ssd_conv_w": nrm(ks[25], (DEPTH, SSD_CONV, SSD_CONV_CH), SSD_CONV ** -0.5),
        "ssd_conv_b": nrm(ks[26], (DEPTH, SSD_CONV_CH), 0.02),
        "ssd_dt_bias": dt0 + jnp.log(-jnp.expm1(-dt0)),
        "ssd_a_log": jnp.log(jax.random.uniform(ks[27], (DEPTH, 2, SSD_HEADS), F32, 1.0, 16.0)),
        "ssd_d": 1.0 + 0.1 * jax.random.normal(ks[29], (DEPTH, SSD_HEADS), F32),
        "ssd_norm": gain(ks[30], (DEPTH, BRANCH)),
    }


def reference(x_prompt, x_sample, c, cache_gqa_k, cache_gqa_v, cache_mla_ckv, cache_mla_krope, state_gla, state_ssd,
              c_ctx, w_ada, b_ada, norm_pre, norm_post, w_in, w_out, gqa_q_norm, gqa_k_norm, gla_w_gate, gla_b_gate,
              gla_norm, mla_q_norm, mla_kv_norm, mla_w_uq, mla_w_ukv, ssd_conv_w, ssd_conv_b, ssd_dt_bias, ssd_a_log,
              ssd_d, ssd_norm):
    def layer_params(l):
        return {
            'norm_pre': norm_pre[l], 'norm_post': norm_post[l], 'w_in': w_in[l], 'w_out': w_out[l],
            'gqa_q_norm': gqa_q_norm[l], 'gqa_k_norm': gqa_k_norm[l],
            'gla_w_gate': gla_w_gate[l], 'gla_b_gate': gla_b_gate[l], 'gla_norm': gla_norm[l],
            'mla_q_norm': mla_q_norm[l], 'mla_kv_norm': mla_kv_norm[l], 'mla_w_uq': mla_w_uq[l], 'mla_w_ukv': mla_w_ukv[l],
            'ssd_conv_w': ssd_conv_w[l], 'ssd_conv_b': ssd_conv_b[l], 'ssd_dt_bias': ssd_dt_bias[l],
            'ssd_a_log': ssd_a_log[l], 'ssd_d': ssd_d[l], 'ssd_norm': ssd_norm[l],
        }

    y_prompt = x_prompt
    ctx_out = []
    for l in range(DEPTH):
        mod_ctx = jax.nn.silu(c_ctx) @ w_ada[l] + b_ada[l]
        y_prompt, ctx_t = trunk_layer(y_prompt, mod_ctx, layer_params(l), None)
        ctx_out.append(ctx_t)
    new_gqa_k = jnp.stack([t[0] for t in ctx_out], axis=1)
    new_gqa_v = jnp.stack([t[1] for t in ctx_out], axis=1)
    new_mla_ckv = jnp.stack([t[2] for t in ctx_out], axis=1)
    new_mla_krope = jnp.stack([t[3] for t in ctx_out], axis=1)
    new_state_gla = jnp.stack([t[4] for t in ctx_out], axis=1)
    new_state_ssd = jnp.stack([t[5] for t in ctx_out], axis=1)

    y_sample = x_sample
    for l in range(DEPTH):
        mod_lat = (jax.nn.silu(c) @ w_ada[l] + b_ada[l])[:, None, :]
        cache_l = (cache_gqa_k[:, l], cache_gqa_v[:, l], cache_mla_ckv[:, l], cache_mla_krope[:, l],
                   state_gla[:, l], state_ssd[:, l])
        y_sample = trunk_layer(y_sample, mod_lat, layer_params(l), cache_l)

    return (y_prompt, y_sample, new_gqa_k, new_gqa_v, new_mla_ckv, new_mla_krope, new_state_gla, new_state_ssd)
```

```python
import os
import math
import numpy as np
from contextlib import ExitStack
import concourse.bass as bass
import concourse.mybir as mybir
from concourse.bass_utils import run_bass_kernel_spmd

F32 = mybir.dt.float32
F32R = mybir.dt.float32r
AF = mybir.ActivationFunctionType
ALU = mybir.AluOpType
AX = mybir.AxisListType
EPS = 1e-6
DEBUG = bool(int(os.environ.get("MK_DEBUG", "0")))
MAXPH = int(os.environ.get("MK_MAXPH", "99"))

OFF = {}
_sizes = [("gq", 256), ("gk", 128), ("gv", 128), ("ggate", 256), ("lq", 128), ("lk", 128), ("lv", 256), ("lglr", 32),
          ("lgate", 256), ("mcq", 192), ("mckv", 128), ("mkr", 32), ("mgate", 256), ("sz", 256), ("sx", 256),
          ("sB", 128), ("sC", 128), ("sdt", 8)]
_o = 0
for _n, _s in _sizes:
    OFF[_n] = _o
    _o += _s
assert _o == 2952


def _swap(R):
    nf = R // 4
    return np.concatenate([np.arange(nf, 2 * nf), np.arange(0, nf), np.arange(3 * nf, 4 * nf), np.arange(2 * nf, 3 * nf)])


def _fm_groups():
    r = np.arange
    g = []
    hq = lambda h: OFF["gq"] + 64 * h + r(64)
    hqs = lambda h: OFF["gq"] + 64 * h + _swap(64)
    g.append(np.concatenate([hq(0), hq(2)]))
    g.append(np.concatenate([hq(1), hq(3)]))
    g.append(OFF["gk"] + r(128))
    gg = lambda h: OFF["ggate"] + 64 * h + r(64)
    g.append(np.concatenate([gg(0), gg(2)]))
    g.append(np.concatenate([gg(1), gg(3)]))
    g.append(OFF["lq"] + r(128))
    g.append(OFF["lk"] + r(128))
    g.append(np.concatenate([OFF["lglr"] + r(32), -np.ones(96, int)]))
    g.append(OFF["lgate"] + r(128))
    g.append(OFF["lgate"] + 128 + r(128))
    g.append(OFF["mcq"] + r(128))
    g.append(np.concatenate([OFF["mcq"] + 128 + r(64), OFF["mkr"] + r(32), OFF["mkr"] + _swap(32)]))
    g.append(OFF["mckv"] + r(128))
    g.append(OFF["mgate"] + r(128))
    g.append(OFF["mgate"] + 128 + r(128))
    g.append(OFF["sz"] + r(128))
    g.append(OFF["sz"] + 128 + r(128))
    g.append(OFF["sx"] + r(128))
    g.append(OFF["sx"] + 128 + r(128))
    g.append(OFF["sB"] + r(128))
    g.append(OFF["sC"] + r(128))
    g.append(np.concatenate([hqs(0), hqs(2)]))
    g.append(np.concatenate([hqs(1), hqs(3)]))
    g.append(np.concatenate([OFF["gk"] + _swap(64), OFF["gk"] + 64 + _swap(64)]))
    return g


FMG = _fm_groups()
NG = len(FMG)
TM0 = np.concatenate([OFF["gv"] + np.arange(128), OFF["lk"] + np.arange(128), OFF["lv"] + np.arange(256)])
TM1 = np.concatenate([OFF["sdt"] + np.arange(8), OFF["mkr"] + np.arange(32), OFF["mckv"] + np.arange(128),
                      OFF["gk"] + np.arange(128)])
NTM1 = 296
UTW = 520


def _take_cols(w, idx):
    out = np.zeros((w.shape[0], len(idx)), np.float32)
    m = idx >= 0
    out[:, m] = w[:, idx[m]]
    return out


def _kpc(w):
    return np.ascontiguousarray(w.reshape(8, 128, w.shape[1]).transpose(1, 0, 2))


class Tk:
    __slots__ = ("t", "name", "w", "rs", "sem", "base", "dcnt", "kind")

    def __init__(self, t, name):
        self.t = t
        self.name = name
        self.w = None
        self.rs = []
        self.sem = None
        self.base = 0
        self.dcnt = 0
        self.kind = None

    def __getitem__(self, k):
        return self.t[k]


class Prog:
    ENG = ("pe", "act", "dve", "pool", "sp")
    CH = 30000

    def __init__(self, nc, es):
        self.nc = nc
        self.es = es
        self.ops = {e: [] for e in self.ENG}
        self.cnt = {e: 0 for e in self.ENG}
        self.esems = {e: [] for e in self.ENG}
        self.seen = {e: {} for e in self.ENG}
        self.pend = {e: [] for e in self.ENG}
        self.free_sems = {"sp": [], "pool": [], "act": []}
        self.scopes = []
        self.alltiles = []
        self.uid = 0
        self.eobj = {"pe": nc.tensor, "act": nc.scalar, "dve": nc.vector, "pool": nc.gpsimd, "sp": nc.sync}

    def sb(self, name, shape, dt=F32):
        self.uid += 1
        st = self.scopes[-1][0] if self.scopes else self.es
        t = st.enter_context(self.nc.sbuf_tensor("%s_%d" % (name, self.uid), list(shape), dt))
        tk = Tk(t, name)
        (self.scopes[-1][1] if self.scopes else self.alltiles).append(tk)
        return tk

    def ps(self, name, shape, dt=F32):
        t = self.es.enter_context(self.nc.psum_tensor(name, list(shape), dt))
        tk = Tk(t, name)
        self.alltiles.append(tk)
        return tk

    def dram(self, name):
        tk = Tk(None, name)
        return tk

    def _newsem(self, name):
        return self.es.enter_context(self.nc.semaphore(name)), 0

    def _esem(self, e, idx):
        k = idx // self.CH
        while len(self.esems[e]) <= k:
            self.esems[e].append(self._newsem("s_%s_%d" % (e, len(self.esems[e])))[0])
        return self.esems[e][k], idx % self.CH + 1

    def _tok_wait(self, tok):
        if tok[0] == "E":
            return self._esem(tok[1], tok[2])
        return tok[1], tok[2]

    def push(self):
        self.scopes.append((ExitStack(), []))

    def pop(self):
        st, tiles = self.scopes.pop()
        waits = []
        for e in self.ENG:
            if e != "sp" and self.cnt[e] > 0:
                waits.append(self._esem(e, self.cnt[e] - 1))
        for tk in tiles:
            if tk.sem is not None:
                waits.append((tk.sem, tk.base + 16 * tk.dcnt))
                self.free_sems[tk.kind].append((tk.sem, tk.base + 16 * tk.dcnt))
        for tk in self.alltiles:
            if tk.sem is not None:
                waits.append((tk.sem, tk.base + 16 * tk.dcnt))
        for e in self.ENG:
            self.pend[e].extend(waits)
        st.close()

    def op(self, eng, fn, reads=(), writes=(), dma=None):
        deps = []
        for r in reads:
            if r.w is not None:
                deps.append((r.w, True))
        for w in writes:
            if w.w is not None and not (dma is not None and w.w[0] == "D" and w.w[3] is dma):
                deps.append((w.w, False))
            for t in w.rs:
                deps.append((t, False))
        waits = {}

        def addw(sem, val):
            key = id(sem)
            if self.seen[eng].get(key, 0) >= val:
                return
            if key not in waits or waits[key][1] < val:
                waits[key] = (sem, val)

        for tok, raw in deps:
            if tok[0] == "E" and tok[1] == eng:
                if eng == "pe" or eng == "sp":
                    continue
            sem, val = self._tok_wait(tok)
            addw(sem, val)
        for sem, val in self.pend[eng]:
            addw(sem, val)
        self.pend[eng] = []
        for key, (sem, val) in waits.items():
            self.seen[eng][key] = val
        if dma is not None:
            assert dma.kind is None or dma.kind == eng, (dma.name, dma.kind, eng)
            if dma.sem is None:
                dma.kind = eng
                if self.free_sems[eng]:
                    dma.sem, dma.base = self.free_sems[eng].pop()
                else:
                    dma.sem, dma.base = self._newsem("d%d" % self.uid)
                self.uid += 1
            dma.dcnt += 1
            tok = ("D", dma.sem, dma.base + 16 * dma.dcnt, dma)
            inc = (dma.sem, 16)
        else:
            idx = self.cnt[eng]
            self.cnt[eng] += 1
            tok = ("E", eng, idx)
            sem, val = self._esem(eng, idx)
            inc = (sem, 1)
        eo = self.eobj[eng]
        for sem, val in waits.values():
            eo.wait_ge(sem, val)
        ins = fn(eo)
        ins.then_inc(inc[0], inc[1])
        for w in writes:
            w.w = tok
            w.rs = []
        for r in reads:
            if r not in writes:
                r.rs.append(tok)
        return tok

    def emit(self):
        fin = []
        for tk in self.alltiles:
            if tk.sem is not None:
                fin.append((tk.sem, tk.base + 16 * tk.dcnt))
        for lst in self.free_sems.values():
            for sem, val in lst:
                fin.append((sem, val))
        for e in self.ENG:
            if e != "sp" and self.cnt[e] > 0:
                fin.append(self._esem(e, self.cnt[e] - 1))
        for sem, val in fin:
            self.nc.sync.wait_ge(sem, val)


def build_program():
    nc = bass.Bass("TRN2", target_bir_lowering=False)
    es = ExitStack()
    P = Prog(nc, es)

    def din(name, shape):
        return nc.dram_tensor(name, list(shape), F32, kind="ExternalInput").ap()

    def dout(name, shape):
        return nc.dram_tensor(name, list(shape), F32, kind="ExternalOutput").ap()

    def dscr(name, shape):
        return nc.dram_tensor(name, list(shape), F32, kind="ExternalOutput" if DEBUG else "Internal").ap()

    x_p = din("x_p", [1024, 1024])
    x_s = din("x_s", [2048, 1024])
    cfm_d = din("cfm", [128, 8, 2])
    wada_fm_d = din("wada_fm", [2, 16, 128, 8, 128])
    wada_g_d = din("wada_g", [2, 128, 8, 1024])
    bada_fm_d = din("bada_fm", [2, 128, 16])
    bada_g_d = din("bada_g", [2, 2, 1024])
    npre_d = din("npre", [2, 128, 8])
    npost_d = din("npost", [2, 2, 1024])
    wfm_d = din("wfm", [2, NG, 128, 8, 128])
    wtm0_d = din("wtm0", [2, 128, 8, 512])
    wtm1_d = din("wtm1", [2, 128, 8, NTM1])
    wout_d = din("wout", [2, 128, 8, 1024])
    cst_d = din("cst", [128, 10, 128])
    rope64_d = din("rope64", [2, 128, 2048])
    rope32_d = din("rope32", [2, 128, 2048])
    vecfm_d = din("vecfm", [2, 128, 24])
    vectm_d = din("vectm", [2, 128, 384])
    wg_d = din("wg", [2, 2, 33, 128])
    wuq_d = din("wuq", [2, 2, 128, 2, 384])
    wukv_d = din("wukv", [2, 128, 512])
    c_gk_d = din("c_gk", [2, 512, 128])
    c_gv_d = din("c_gv", [2, 512, 128])
    c_ckv_d = din("c_ckv", [2, 512, 128])
    c_kr_d = din("c_kr", [2, 512, 32])
    s_gla_d = din("s_gla", [2, 2, 128, 64])
    s_ssd_d = din("s_ssd", [2, 2, 256, 64])
    y_p = dout("y_p", [1024, 1024])
    y_s = dout("y_s", [2048, 1024])
    o_gk = dout("o_gk", [4, 2, 256, 128])
    o_gv = dout("o_gv", [4, 2, 256, 128])
    o_ckv = dout("o_ckv", [4, 2, 256, 128])
    o_kr = dout("o_kr", [4, 2, 256, 32])
    o_gla = dout("o_gla", [4, 2, 2, 128, 64])
    o_ssd = dout("o_ssd", [4, 2, 2, 256, 64])
    U_fm = dscr("U_fm", [NG * 128, 2048])
    U_tm = dscr("U_tm", [2048, UTW])
    O_fm = dscr("O_fm", [1024, 2048])
    Y0 = dscr("Y0", [2048, 1024])
    dU_fm, dU_tm, dO_fm, dY0 = P.dram("U_fm"), P.dram("U_tm"), P.dram("O_fm"), P.dram("Y0")
    dOUT = P.dram("outs")

    def MM(o, o_ap, l, l_ap, r, r_ap, st=True, sp=True, tp=None, sg=False):
        kw = {}
        if tp is not None:
            kw["tile_position"] = tp
        if sg:
            kw["skip_group_check"] = True
        P.op("pe", lambda e: e.matmul(o_ap, lhsT=l_ap, rhs=r_ap, start=st, stop=sp, **kw), reads=[l, r], writes=[o])

    def TR(o, o_ap, a, a_ap, idn_ap):
        P.op("pe", lambda e: e.transpose(out=o_ap, in_=a_ap, identity=idn_ap), reads=[a, CST], writes=[o])

    def TT(o, o_ap, a, a_ap, b, b_ap, op, eng="dve"):
        P.op(eng, lambda e: e.tensor_tensor(out=o_ap, in0=a_ap, in1=b_ap, op=op), reads=[a, b], writes=[o])

    def TS(o, o_ap, a, a_ap, s1, op0, s2=None, op1=None, rd=(), eng="dve"):
        if op1 is None:
            P.op(eng, lambda e: e.tensor_scalar(out=o_ap, in0=a_ap, scalar1=s1, scalar2=None, op0=op0),
                 reads=[a] + list(rd), writes=[o])
        else:
            P.op(eng, lambda e: e.tensor_scalar(out=o_ap, in0=a_ap, scalar1=s1, scalar2=s2, op0=op0, op1=op1),
                 reads=[a] + list(rd), writes=[o])

    def STT(o, o_ap, a, a_ap, s, b, b_ap, op0, op1, rd=(), eng="dve"):
        P.op(eng, lambda e: e.scalar_tensor_tensor(out=o_ap, in0=a_ap, scalar=s, in1=b_ap, op0=op0, op1=op1),
             reads=[a, b] + list(rd), writes=[o])

    def ACT(o, o_ap, a, a_ap, func, bias=None, scale=None, rd=()):
        kw = {}
        if bias is not None:
            kw["bias"] = bias
        if scale is not None:
            kw["scale"] = scale
        P.op("act", lambda e: e.activation(out=o_ap, in_=a_ap, func=func, **kw), reads=[a] + list(rd), writes=[o])

    def CP(o, o_ap, a, a_ap, eng="dve"):
        if eng == "act":
            P.op("act", lambda e: e.copy(out=o_ap, in_=a_ap), reads=[a], writes=[o])
        else:
            P.op(eng, lambda e: e.tensor_copy(out=o_ap, in_=a_ap), reads=[a], writes=[o])

    def MEMSET(o, o_ap, val, eng="dve"):
        P.op(eng, lambda e: e.memset(o_ap, val), writes=[o])

    def RSUM(o, o_ap, a, a_ap):
        P.op("dve", lambda e: e.reduce_sum(out=o_ap, in_=a_ap, axis=AX.X), reads=[a], writes=[o])

    def RECIP(o, o_ap, a, a_ap):
        P.op("dve", lambda e: e.reciprocal(out=o_ap, in_=a_ap), reads=[a], writes=[o])

    def LD(tk, o_ap, src_ap, dr=None):
        P.op("sp", lambda e: e.dma_start(out=o_ap, in_=src_ap), reads=([dr] if dr is not None else []), writes=[tk], dma=tk)

    def ST(dst_ap, tk, i_ap, dw=None):
        P.op("pool", lambda e: e.dma_start(out=dst_ap, in_=i_ap), reads=[tk], writes=([dw] if dw is not None else []), dma=tk)

    PSB = [P.ps("psb%d" % i, [128, 512]) for i in range(8)]
    psi = [0]

    def bank():
        b = PSB[psi[0] % 4]
        psi[0] += 1
        return b
    pacc = [0]

    def bank_acc():
        b = PSB[4 + pacc[0] % 4]
        pacc[0] += 1
        return b

    CST = P.sb("CST", [128, 10, 128])
    LD(CST, CST[:, :, :], cst_d[:, :, :])
    IDN = lambda n=128: CST[0:n, 0, 0:n]
    ONES, TRIF, TRIB, SUF, SLB, BO64, HM32 = 1, 2, 3, 4, 5, 8, 9
    VFM = [P.sb("VFM%d" % l, [128, 24]) for l in range(2)]
    VTM = [P.sb("VTM%d" % l, [128, 384]) for l in range(2)]
    for l in range(2):
        LD(VFM[l], VFM[l][:, :], vecfm_d[l])
        LD(VTM[l], VTM[l][:, :], vectm_d[l])
    CW = [P.sb("CW%d" % l, [128, 4, 5]) for l in range(2)]
    cw_d = din("cw", [2, 128, 4, 5])
    for l in range(2):
        LD(CW[l], CW[l][:, :, :], cw_d[l])
    MODA = P.sb("MODA", [128, 2, 8, 2])
    MODB = P.sb("MODB", [128, 2, 8, 2])
    GBC = [[P.sb("GBC%d%d" % (l, w), [128, 1024]) for w in range(2)] for l in range(2)]
    NEGA = [P.sb("NEGA%d" % l, [128, 8]) for l in range(2)]

    P.push()
    CF = P.sb("CF", [128, 8, 2])
    LD(CF, CF[:, :, :], cfm_d[:, :, :])
    SC = P.sb("SC", [128, 8, 2])
    ACT(SC, SC[:, :, :], CF, CF[:, :, :], AF.Silu)
    for l in range(2):
        BF = P.sb("BF", [128, 16])
        LD(BF, BF[:, :], bada_fm_d[l])
        NP_ = P.sb("NP", [128, 8])
        LD(NP_, NP_[:, :], npre_d[l])
        MF = P.sb("MF", [128, 16, 2])
        WAs = [P.sb("WA%d" % i, [128, 8, 128]) for i in range(3)]
        for g in range(16):
            W = WAs[g % 3]
            LD(W, W[:, :, :], wada_fm_d[l, g])
            pb = bank()
            for k in range(8):
                MM(pb, pb[:, 0:2], W, W[:, k, :], SC, SC[:, k, :], st=(k == 0), sp=(k == 7))
            TS(MF, MF[:, g, :], pb, pb[:, 0:2], BF[:, g:g + 1], ALU.add, rd=[BF])
        for w in range(2):
            TS(MODA, MODA[:, l, :, w], MF, MF[:, 8:16, w], 1.0, ALU.add)
            TT(MODA, MODA[:, l, :, w], MODA, MODA[:, l, :, w], NP_, NP_[:, :], ALU.mult)
            CP(MODB, MODB[:, l, :, w], MF, MF[:, 0:8, w])
        WGt = P.sb("WGt", [128, 8, 1024])
        LD(WGt, WGt[:, :, :], wada_g_d[l])
        BG = P.sb("BG", [2, 1024])
        LD(BG, BG[:, :], bada_g_d[l])
        NPO = P.sb("NPO", [2, 1024])
        LD(NPO, NPO[:, :], npost_d[l])
        GN = P.sb("GN", [2, 1024])
        for hf in range(2):
            pb = bank()
            for k in range(8):
                MM(pb, pb[0:2, :], SC, SC[:, k, :], WGt, WGt[:, k, hf * 512:(hf + 1) * 512], st=(k == 0), sp=(k == 7))
            TT(GN, GN[:, hf * 512:(hf + 1) * 512], pb, pb[0:2, :], BG, BG[:, hf * 512:(hf + 1) * 512], ALU.add)
        TT(GN, GN[:, :], GN, GN[:, :], NPO, NPO[:, :], ALU.mult)
        for w in range(2):
            for hf in range(2):
                pb = bank()
                MM(pb, pb[:, :], CST, CST[0:2, 6 + w, :], GN, GN[0:2, hf * 512:(hf + 1) * 512])
                CP(GBC[l][w], GBC[l][w][:, hf * 512:(hf + 1) * 512], pb, pb[:, :], eng="act")
        ACT(NEGA[l], NEGA[l][:, :], VTM[l], VTM[l][:, 200:208], AF.Exp)
        TS(NEGA[l], NEGA[l][:, :], NEGA[l], NEGA[l][:, :], -1.0, ALU.mult)
    P.pop()

    def run_layer(job, l):
        ctx = (job == 0)
        who = 0 if ctx else 1
        T = 1024 if ctx else 2048
        nseq, L = (4, 256) if ctx else (1, 2048)
        NT = T // 128
        if l == 0:
            xsrc, dxs = (x_p if ctx else x_s), None
        else:
            xsrc, dxs = Y0, dY0
        if l == 1:
            ydst, dyd = (y_p if ctx else y_s), dOUT
        else:
            ydst, dyd = Y0, dY0
        koff = 0 if ctx else 512
        nk = koff + L
        nkt = nk // 128
        groups = list(range(21)) + ([] if ctx else [21, 22, 23])
        vf, vt = VFM[l], VTM[l]

        def rstd_from(o, o_ap, a, a_ap, n):
            ACT(o, o_ap, a, a_ap, AF.Ln, bias=EPS_T[0:a_ap.shape[0], 0:1], scale=1.0 / n, rd=[EPS_T])
            ACT(o, o_ap, o, o_ap, AF.Exp, scale=-0.5)

        P.push()
        EPS_T = P.sb("EPS", [128, 1])
        MEMSET(EPS_T, EPS_T[:, :], EPS)
        TH = min(T, 1024)
        NTh = TH // 128
        HTh = P.sb("HTh", [128, 8, TH], F32R)
        HTl = P.sb("HTl", [128, 8, TH], F32R)
        HTt = [P.sb("HTt%d" % i, [128, 8, 128]) for i in range(2)]
        XT = [P.sb("XT%d" % i, [128, 1024]) for i in range(2)]
        SQ = P.sb("SQ", [128, 1024])
        ST1 = [P.sb("ST1_%d" % i, [128, 2]) for i in range(2)]
        nw1 = NTM1 if ctx else 8

        def split(hi, hi_ap, lo, lo_ap, src, src_ap):
            CP(hi, hi_ap, src, src_ap)
            TT(lo, lo_ap, src, src_ap, hi, hi_ap.bitcast(F32), ALU.subtract)

        def mm3(pb, o_ap, ah, al, a_sl, bh, bl, b_sl):
            n = 0
            for k in range(8):
                for (x, y) in ((ah, bh), (ah, bl), (al, bh)):
                    MM(pb, o_ap, x, a_sl(x, k), y, b_sl(y, k), st=(n == 0), sp=(n == 23))
                    n += 1

        W0h = P.sb("W0h", [128, 8, 512], F32R)
        W0l = P.sb("W0l", [128, 8, 512], F32R)
        W1h = P.sb("W1h", [128, 8, nw1], F32R)
        W1l = P.sb("W1l", [128, 8, nw1], F32R)
        TG = [P.sb("TG%d" % i, [128, 520]) for i in range(2)]
        TQ = [P.sb("TQ%d" % i, [128, 300]) for i in range(2)]

        def build_ht(half, tl):
            t = half * NTh + tl
            X = XT[t % 2]
            s1 = ST1[t % 2]
            Ht = HTt[t % 2]
            LD(X, X[:, :], xsrc[t * 128:(t + 1) * 128, :], dr=dxs)
            ACT(SQ, SQ[:, :], X, X[:, :], AF.Square)
            RSUM(s1, s1[:, 0:1], SQ, SQ[:, :])
            rstd_from(s1, s1[:, 1:2], s1, s1[:, 0:1], 1024.0)
            TS(X, X[:, :], X, X[:, :], s1[:, 1:2], ALU.mult, rd=[s1])
            for hf in range(2):
                pb = bank()
                for kk in range(4):
                    k = hf * 4 + kk
                    TR(pb, pb[:, kk * 128:(kk + 1) * 128], X, X[:, k * 128:(k + 1) * 128], IDN())
                for kk in range(4):
                    k = hf * 4 + kk
                    if kk % 2 == 0:
                        TS(Ht, Ht[:, k, :], pb, pb[:, kk * 128:(kk + 1) * 128],
                           MODA[:, l, k, who:who + 1], ALU.mult, MODB[:, l, k, who:who + 1], ALU.add, rd=[MODA, MODB])
                    else:
                        ACT(Ht, Ht[:, k, :], pb, pb[:, kk * 128:(kk + 1) * 128], AF.Identity,
                            bias=MODB[:, l, k, who:who + 1], scale=MODA[:, l, k, who:who + 1], rd=[MODA, MODB])
            cs_ = slice(tl * 128, (tl + 1) * 128)
            split(HTh, HTh[:, :, cs_], HTl, HTl[:, :, cs_], Ht, Ht[:, :, :])

        def tm_mm(half, tl):
            cs_ = slice(tl * 128, (tl + 1) * 128)
            pb = bank_acc()
            mm3(pb, pb[:, :], HTh, HTl, lambda x, k: x[:, k, cs_], W0h, W0l, lambda y, k: y[:, k, :])
            pb2 = bank_acc()
            mm3(pb2, pb2[:, 0:nw1], HTh, HTl, lambda x, k: x[:, k, cs_], W1h, W1l, lambda y, k: y[:, k, 0:nw1])
            return pb, pb2

        def tm_evac(half, tl, pbs_):
            pb, pb2 = pbs_
            t = half * NTh + tl
            G = TG[t % 2]
            Q = TQ[t % 2]
            CP(G, G[:, 0:512], pb, pb[:, :], eng="act")
            CP(G, G[:, 512:520], pb2, pb2[:, 0:8])
            ST(U_tm[t * 128:(t + 1) * 128, :], G, G[:, :], dw=dU_tm)
            if ctx:
                s, tt_ = t // 2, t % 2
                rows = slice(tt_ * 128, (tt_ + 1) * 128)
                ST(o_gv[s, l, rows, :], G, G[:, 0:128], dw=dOUT)
                CP(Q, Q[:, 0:32], pb2, pb2[:, 8:40])
                ST(o_kr[s, l, rows, :], Q, Q[:, 0:32], dw=dOUT)
                ACT(Q, Q[:, 32:160], pb2, pb2[:, 40:168], AF.Square)
                RSUM(Q, Q[:, 296:297], Q, Q[:, 32:160])
                rstd_from(Q, Q[:, 297:298], Q, Q[:, 296:297], 128.0)
                STT(Q, Q[:, 32:160], pb2, pb2[:, 40:168], Q[:, 297:298], vt, vt[:, 0:128], ALU.mult, ALU.mult)
                ST(o_ckv[s, l, rows, :], Q, Q[:, 32:160], dw=dOUT)
                ACT(Q, Q[:, 160:288], pb2, pb2[:, 168:296], AF.Square)
                RSUM(Q, Q[:, 298:300], Q, Q[:, 160:288].rearrange("p (a b) -> p a b", a=2))
                rstd_from(Q, Q[:, 298:300], Q, Q[:, 298:300], 64.0)
                TT(Q, Q[:, 160:288].rearrange("p (a b) -> p a b", a=2), pb2, pb2[:, 168:296].rearrange("p (a b) -> p a b", a=2),
                   Q, Q[:, 298:300].unsqueeze(2).to_broadcast([128, 2, 64]), ALU.mult)
                TT(Q, Q[:, 160:288].rearrange("p (a b) -> p a b", a=2), Q, Q[:, 160:288].rearrange("p (a b) -> p a b", a=2),
                   vt, vt[:, 128:192].unsqueeze(1).to_broadcast([128, 2, 64]), ALU.mult)
                ST(o_gk[s, l, rows, :], Q, Q[:, 160:288], dw=dOUT)

        P.push()
        WT0 = P.sb("WT0", [128, 8, 512])
        WT1 = P.sb("WT1", [128, 8, NTM1])
        LD(WT0, WT0[:, :, :], wtm0_d[l])
        LD(WT1, WT1[:, :, :], wtm1_d[l])
        build_ht(0, 0)
        split(W0h, W0h[:, :, :], W0l, W0l[:, :, :], WT0, WT0[:, :, :])
        split(W1h, W1h[:, :, 0:nw1], W1l, W1l[:, :, 0:nw1], WT1, WT1[:, :, 0:nw1])
        P.pop()
        WF = [P.sb("WF%d" % i, [128, 8, 128]) for i in range(2)]
        WFh = [P.sb("WFh%d" % i, [128, 8, 128], F32R) for i in range(2)]
        WFl = [P.sb("WFl%d" % i, [128, 8, 128], F32R) for i in range(2)]
        SG = [P.sb("SG%d" % i, [128, 512]) for i in range(3)]

        def prep_w(gi):
            W = WF[gi % 2]
            LD(W, W[:, :, :], wfm_d[l, groups[gi]])
            split(WFh[gi % 2], WFh[gi % 2][:, :, :], WFl[gi % 2], WFl[gi % 2][:, :, :], W, W[:, :, :])

        for half in range(T // TH):
            tok0 = half * TH
            prep_w(0)
            if True:
                if half > 0:
                    build_ht(half, 0)
                prev_ = None
                for tl in range(NTh):
                    if tl + 1 < NTh:
                        build_ht(half, tl + 1)
                    if prev_ is not None:
                        tm_evac(half, prev_[0], prev_[1])
                    prev_ = (tl, tm_mm(half, tl))
                tm_evac(half, prev_[0], prev_[1])
            si = 0
            for gi, g in enumerate(groups):
                if gi + 1 < len(groups):
                    prep_w(gi + 1)
                Wh_, Wl_ = WFh[gi % 2], WFl[gi % 2]
                for b in range(TH // 512):
                    bs_ = slice(b * 512, (b + 1) * 512)
                    pb = bank()
                    mm3(pb, pb[:, :], Wh_, Wl_, lambda x, k: x[:, k, :], HTh, HTl, lambda y, k, bs_=bs_: y[:, k, bs_])
                    S = SG[si % 3]
                    si += 1
                    CP(S, S[:, :], pb, pb[:, :], eng=("act" if si % 2 else "dve"))
                    ST(U_fm[g * 128:(g + 1) * 128, tok0 + b * 512:tok0 + (b + 1) * 512], S, S[:, :], dw=dU_fm)
        P.pop()
        if MAXPH < 2:
            return

        def attention(KT, kbase, kdim, QTt, qbase, qc0, nq, VAt, vcol, scale, gate_tk, gate_ap, orow, ocol0, PTs, misc, pend):
            OA = bank_acc()
            sb_ = [None] * nkt

            def S(kt):
                pb = bank()
                MM(pb, pb[:, 0:nq], KT, KT[kbase:kbase + kdim, kt * 128:(kt + 1) * 128],
                   QTt, QTt[qbase:qbase + kdim, qc0:qc0 + nq])
                sb_[kt] = pb
            S(0)
            if nkt > 1:
                S(1)
            for kt in range(nkt):
                if kt + 2 < nkt:
                    S(kt + 2)
                if kt == min(1, nkt - 1) and pend:
                    pend.pop()()
                PT = PTs[kt % len(PTs)]
                ACT(PT, PT[:, 0:nq], sb_[kt], sb_[kt][:, 0:nq], AF.Exp, scale=scale)
                MM(OA, OA[0:65, 0:nq], VAt, VAt[:, kt, vcol:vcol + 65], PT, PT[:, 0:nq], st=(kt == 0), sp=(kt == nkt - 1))
            mi = misc[0]
            misc[0] = (mi + 1) % 2
            RD, T1, OS = misc[1][mi]
            RECIP(RD, RD[64:65, 0:nq], OA, OA[64:65, 0:nq])

            def epi():
                pb = bank()
                MM(pb, pb[0:64, 0:nq], CST, CST[64:65, ONES, 0:64], RD, RD[64:65, 0:nq])
                TT(T1, T1[0:64, 0:nq], pb, pb[0:64, 0:nq], gate_tk, gate_ap, ALU.mult)
                TT(OS, OS[0:64, 0:nq], OA, OA[0:64, 0:nq], T1, T1[0:64, 0:nq], ALU.mult)
                ST(O_fm[orow:orow + 64, ocol0:ocol0 + nq], OS, OS[0:64, 0:nq], dw=dO_fm)
            pend.append(epi)

        def load_fm(tk, ap, g, c0, n, r0=0, r1=128):
            LD(tk, ap, U_fm[g * 128 + r0:g * 128 + r1, c0:c0 + n], dr=dU_fm)

        P.push()
        EPS_T = P.sb("EPS", [128, 1])
        MEMSET(EPS_T, EPS_T[:, :], EPS)
        if not ctx:
            ROPE = P.sb("ROPE", [128, 2, 2048])
            LD(ROPE, ROPE[:, 0, :], rope64_d[0])
            LD(ROPE, ROPE[:, 1, :], rope64_d[1])
        nset = 2 if nseq > 1 else 1
        sets2 = []
        for si_ in range(nset):
            st_ = {}
            st_["QT"] = [P.sb("QT%d_%d" % (i, si_), [128, L]) for i in range(2)]
            st_["QZ"] = [P.sb("QZ%d_%d" % (i, si_), [128, L]) for i in range(4)]
            for h in range(4):
                zb = 64 * (1 - h // 2)
                MEMSET(st_["QZ"][h], st_["QZ"][h][zb:zb + 64, :], 0.0, eng="pool")
            st_["KT"] = P.sb("KT_%d" % si_, [128, nk])
            st_["GA"] = [P.sb("GA%d_%d" % (i, si_), [128, L]) for i in range(2)]
            st_["VA"] = P.sb("VA_%d" % si_, [128, nkt, 130])
            sets2.append(st_)
        XS_ = P.sb("XS", [128, min(L, 512)])
        TMPA = P.sb("TMPA", [128, min(L, 512)])
        TMPB = P.sb("TMPB", [128, min(L, 512)])
        RS = P.sb("RS", [128, min(L, 512)])
        PTs = [P.sb("PT%d" % i, [128, 512]) for i in range(3)]
        misc = [0, [(P.sb("RD%d" % i, [128, 512]), P.sb("T1%d" % i, [64, 512]), P.sb("OS%d" % i, [64, 512])) for i in range(2)]]
        pend = []
        CTM = P.sb("CTM", [128, 128])

        def p2_pro(s, st_):
            QT, QZ, KTt, GA, VA = st_["QT"], st_["QZ"], st_["KT"], st_["GA"], st_["VA"]
            t0 = s * L
            MEMSET(VA, VA[:, :, :], 1.0)
            for c in range(2):
                load_fm(QT[c], QT[c][:, :], c, t0, L)
                load_fm(GA[c], GA[c][:, :], 3 + c, t0, L)
                ACT(GA[c], GA[c][:, :], GA[c], GA[c][:, :], AF.Silu)
            load_fm(KTt, KTt[:, koff:koff + L], 2, t0, L)
            for kt in range(L // 128):
                LD(VA, VA[:, koff // 128 + kt, :].rearrange("p (a b) -> p a b", a=2)[:, :, 0:64],
                   U_tm[t0 + kt * 128:t0 + (kt + 1) * 128, 0:128].rearrange("p (a b) -> p a b", a=2), dr=dU_tm)
            if not ctx:
                for kt in range(4):
                    LD(VA, VA[:, kt, :].rearrange("p (a b) -> p a b", a=2)[:, :, 0:64],
                       c_gv_d[l, kt * 128:(kt + 1) * 128, :].rearrange("p (a b) -> p a b", a=2))
                    LD(CTM, CTM[:, :], c_gk_d[l, kt * 128:(kt + 1) * 128, :])
                    pb = bank()
                    TR(pb, pb[:, 0:128], CTM, CTM[:, :], IDN())
                    CP(KTt, KTt[:, kt * 128:(kt + 1) * 128], pb, pb[:, 0:128])
            for (tk, c0, gcol, gsw) in [(QT[0], 0, 0, 21), (QT[1], 0, 0, 22), (KTt, koff, 2, 23)]:
                for p0 in range(0, L, 512):
                    n = min(512, L - p0)
                    ap = tk[:, c0 + p0:c0 + p0 + n]
                    ACT(TMPA, TMPA[:, 0:n], tk, ap, AF.Square)
                    pb = bank()
                    MM(pb, pb[:, 0:n], CST, CST[:, BO64, :], TMPA, TMPA[:, 0:n])
                    rstd_from(RS, RS[:, 0:n], pb, pb[:, 0:n], 64.0)
                    if tk is KTt:
                        dsts = [(KTt, 0, 128, ap)]
                    else:
                        ci = 0 if tk is QT[0] else 1
                        dsts = [(QZ[ci], 0, 64, QZ[ci][0:64, p0:p0 + n]), (QZ[ci + 2], 64, 128, QZ[ci + 2][64:128, p0:p0 + n])]
                    if ctx:
                        for (dt_, r0, r1, dap) in dsts:
                            STT(dt_, dap, tk, tk[r0:r1, c0 + p0:c0 + p0 + n], vf[r0:r1, gcol:gcol + 1], RS, RS[r0:r1, 0:n], ALU.mult, ALU.mult, rd=[vf])
                    else:
                        load_fm(XS_, XS_[:, 0:n], gsw, t0 + p0, n)
                        STT(TMPA, TMPA[:, 0:n], tk, ap, vf[:, gcol:gcol + 1], ROPE, ROPE[:, 0, p0:p0 + n], ALU.mult, ALU.mult, rd=[vf])
                        STT(TMPB, TMPB[:, 0:n], XS_, XS_[:, 0:n], vf[:, gcol + 1:gcol + 2], ROPE, ROPE[:, 1, p0:p0 + n], ALU.mult, ALU.mult, rd=[vf])
                        TT(TMPA, TMPA[:, 0:n], TMPA, TMPA[:, 0:n], TMPB, TMPB[:, 0:n], ALU.add)
                        for (dt_, r0, r1, dap) in dsts:
                            TT(dt_, dap, TMPA, TMPA[r0:r1, 0:n], RS, RS[r0:r1, 0:n], ALU.mult)

        def p2_main(s, st_):
            QZ, KTt, GA, VA = st_["QZ"], st_["KT"], st_["GA"], st_["VA"]
            t0 = s * L
            for h in range(4):
                c, base = h % 2, 64 * (h // 2)
                for q0 in range(0, L, 512):
                    nq = min(512, L - q0)
                    attention(KTt, 0, 128, QZ[h], 0, q0, nq, VA, (h // 2) * 65, 0.125,
                              GA[c], GA[c][base:base + 64, q0:q0 + nq], h * 64, t0 + q0, PTs, misc, pend)
            while pend:
                pend.pop()()

        p2_pro(0, sets2[0])
        for s in range(nseq):
            if s + 1 < nseq:
                p2_pro(s + 1, sets2[(s + 1) % nset])
            p2_main(s, sets2[s % nset])
        P.pop()
        if MAXPH < 3:
            return

        P.push()
        EPS_T = P.sb("EPS", [128, 1])
        MEMSET(EPS_T, EPS_T[:, :], EPS)
        if not ctx:
            ROPE = P.sb("ROPE", [128, 2, 2048])
            LD(ROPE, ROPE[64:96, 0, :], rope32_d[0, 64:96, :])
            LD(ROPE, ROPE[64:96, 1, :], rope32_d[1, 64:96, :])
        WUQ = P.sb("WUQ", [128, 2, 384])
        LD(WUQ, WUQ[:, :, :], wuq_d[l, 0])
        WUQS = P.sb("WUQS", [128, 2, 384])
        LD(WUQS, WUQS[:, :, :], wuq_d[l, 1])
        WUKV = P.sb("WUKV", [128, 512])
        LD(WUKV, WUKV[:, :], wukv_d[l])
        WUV = P.sb("WUV", [128, 256])
        for h in range(4):
            CP(WUV, WUV[:, h * 64:(h + 1) * 64], WUKV, WUKV[:, h * 128 + 64:h * 128 + 128])
        nset = 2 if nseq > 1 else 1
        sets3 = []
        for si_ in range(nset):
            st_ = {}
            st_["CQA"] = P.sb("CQA_%d" % si_, [128, L])
            st_["G11"] = P.sb("G11_%d" % si_, [128, L])
            st_["CKV"] = P.sb("CKV_%d" % si_, [128, nk])
            st_["GC"] = [P.sb("GC%d_%d" % (i, si_), [128, L]) for i in range(2)]
            st_["KM"] = [P.sb("KM%d_%d" % (i, si_), [96, nk]) for i in range(4)]
            st_["VC"] = P.sb("VC_%d" % si_, [128, nkt, 260])
            st_["QM"] = [P.sb("QM%d_%d" % (i, si_), [96, 512]) for i in range(4)]
            sets3.append(st_)
        TMPA = P.sb("TMPA", [128, 512])
        TMPB = P.sb("TMPB", [128, 512])
        RS = P.sb("RS", [128, 512])
        PTs = [P.sb("PT%d" % i, [128, 512]) for i in range(3)]
        misc = [0, [(P.sb("RD%d" % i, [128, 512]), P.sb("T1%d" % i, [64, 512]), P.sb("OS%d" % i, [64, 512])) for i in range(2)]]
        pend = []
        CTM = P.sb("CTM", [128, 160])

        def p3_pro(s, st_):
            CQA, G11, CKV, GC, KM, VC = st_["CQA"], st_["G11"], st_["CKV"], st_["GC"], st_["KM"], st_["VC"]
            t0 = s * L
            MEMSET(VC, VC[:, :, :], 1.0)
            load_fm(CQA, CQA[:, :], 10, t0, L)
            load_fm(G11, G11[:, :], 11, t0, L)
            load_fm(CKV, CKV[:, koff:koff + L], 12, t0, L)
            for c in range(2):
                load_fm(GC[c], GC[c][:, :], 13 + c, t0, L)
                ACT(GC[c], GC[c][:, :], GC[c], GC[c][:, :], AF.Silu)
            if not ctx:
                for kt in range(4):
                    LD(CTM, CTM[:, 0:128], c_ckv_d[l, kt * 128:(kt + 1) * 128, :])
                    LD(CTM, CTM[:, 128:160], c_kr_d[l, kt * 128:(kt + 1) * 128, :])
                    pb = bank()
                    TR(pb, pb[:, 0:128], CTM, CTM[:, 0:128], IDN())
                    CP(CKV, CKV[:, kt * 128:(kt + 1) * 128], pb, pb[:, 0:128])
                    pb = bank()
                    TR(pb, pb[0:32, 0:128], CTM, CTM[:, 128:160], IDN())
                    for h in range(4):
                        CP(KM[h], KM[h][64:96, kt * 128:(kt + 1) * 128], pb, pb[0:32, 0:128], eng="dve")
            for p0 in range(0, L, 512):
                n = min(512, L - p0)
                ap = CKV[:, koff + p0:koff + p0 + n]
                ACT(TMPA, TMPA[:, 0:n], CKV, ap, AF.Square)
                pb = bank()
                MM(pb, pb[:, 0:n], CST, CST[:, ONES, :], TMPA, TMPA[:, 0:n])
                rstd_from(RS, RS[:, 0:n], pb, pb[:, 0:n], 128.0)
                STT(CKV, ap, CKV, ap, vf[:, 7:8], RS, RS[:, 0:n], ALU.mult, ALU.mult, rd=[vf])
                if ctx:
                    for h in range(4):
                        CP(KM[h], KM[h][64:96, koff + p0:koff + p0 + n], G11, G11[64:96, p0:p0 + n], eng=("act" if h % 2 else "dve"))
                else:
                    TT(TMPA, TMPA[64:96, 0:n], G11, G11[64:96, p0:p0 + n], ROPE, ROPE[64:96, 0, p0:p0 + n], ALU.mult)
                    load_fm(TMPB, TMPB[64:96, 0:n], 11, t0 + p0, n, 96, 128)
                    TT(TMPB, TMPB[64:96, 0:n], TMPB, TMPB[64:96, 0:n], ROPE, ROPE[64:96, 1, p0:p0 + n], ALU.mult)
                    for h in range(4):
                        TT(KM[h], KM[h][64:96, koff + p0:koff + p0 + n], TMPA, TMPA[64:96, 0:n], TMPB, TMPB[64:96, 0:n], ALU.add)
            for p0 in range(0, nk, 512):
                n = min(512, nk - p0)
                for h in range(4):
                    pb = bank()
                    MM(pb, pb[0:64, 0:n], WUKV, WUKV[:, h * 128:h * 128 + 64], CKV, CKV[:, p0:p0 + n])
                    CP(KM[h], KM[h][0:64, p0:p0 + n], pb, pb[0:64, 0:n], eng=("act" if h % 2 else "dve"))
            for kt in range(nkt):
                pb = bank()
                MM(pb, pb[:, 0:256], CKV, CKV[:, kt * 128:(kt + 1) * 128], WUV, WUV[:, :])
                CP(VC, VC[:, kt, :].rearrange("p (a b) -> p a b", a=4)[:, :, 0:64],
                   pb, pb[:, 0:256].rearrange("p (a b) -> p a b", a=4), eng=("act" if kt % 2 else "dve"))

        def p3_qpath(s, st_, q0):
            CQA, G11, QM = st_["CQA"], st_["G11"], st_["QM"]
            nq = min(512, L - q0)
            ACT(TMPA, TMPA[:, 0:nq], CQA, CQA[:, q0:q0 + nq], AF.Square)
            ACT(TMPB, TMPB[0:64, 0:nq], G11, G11[0:64, q0:q0 + nq], AF.Square)
            pb = bank()
            MM(pb, pb[:, 0:nq], CST, CST[:, ONES, :], TMPA, TMPA[:, 0:nq], st=True, sp=False)
            MM(pb, pb[:, 0:nq], CST, CST[0:64, ONES, :], TMPB, TMPB[0:64, 0:nq], st=False, sp=True)
            rstd_from(RS, RS[:, 0:nq], pb, pb[:, 0:nq], 192.0)
            STT(TMPA, TMPA[:, 0:nq], CQA, CQA[:, q0:q0 + nq], vf[:, 5:6], RS, RS[:, 0:nq], ALU.mult, ALU.mult, rd=[vf])
            STT(TMPB, TMPB[0:64, 0:nq], G11, G11[0:64, q0:q0 + nq], vf[0:64, 6:7], RS, RS[0:64, 0:nq], ALU.mult, ALU.mult, rd=[vf])
            for h in range(4):
                pb = bank()
                MM(pb, pb[0:96, 0:nq], WUQ, WUQ[:, 0, h * 96:(h + 1) * 96], TMPA, TMPA[:, 0:nq], st=True, sp=False)
                MM(pb, pb[0:96, 0:nq], WUQ, WUQ[0:64, 1, h * 96:(h + 1) * 96], TMPB, TMPB[0:64, 0:nq], st=False, sp=True)
                if ctx:
                    CP(QM[h], QM[h][0:96, 0:nq], pb, pb[0:96, 0:nq], eng="act")
                else:
                    pb2 = bank()
                    MM(pb2, pb2[0:96, 0:nq], WUQS, WUQS[:, 0, h * 96:(h + 1) * 96], TMPA, TMPA[:, 0:nq], st=True, sp=False)
                    MM(pb2, pb2[0:96, 0:nq], WUQS, WUQS[0:64, 1, h * 96:(h + 1) * 96], TMPB, TMPB[0:64, 0:nq], st=False, sp=True)
                    CP(QM[h], QM[h][0:64, 0:nq], pb, pb[0:64, 0:nq], eng="act")
                    TT(QM[h], QM[h][64:96, 0:nq], pb, pb[64:96, 0:nq], ROPE, ROPE[64:96, 0, q0:q0 + nq], ALU.mult)
                    TT(RS, RS[64:96, 0:nq], pb2, pb2[64:96, 0:nq], ROPE, ROPE[64:96, 1, q0:q0 + nq], ALU.mult)
                    TT(QM[h], QM[h][64:96, 0:nq], QM[h], QM[h][64:96, 0:nq], RS, RS[64:96, 0:nq], ALU.add)

        def p3_attn(s, st_, q0):
            GC, KM, VC, QM = st_["GC"], st_["KM"], st_["VC"], st_["QM"]
            nq = min(512, L - q0)
            t0 = s * L
            for h in range(4):
                c, base = h // 2, 64 * (h % 2)
                attention(KM[h], 0, 96, QM[h], 0, 0, nq, VC, h * 65, 96.0 ** -0.5,
                          GC[c], GC[c][base:base + 64, q0:q0 + nq], 512 + h * 64, t0 + q0, PTs, misc, pend)

        if nseq > 1:
            p3_pro(0, sets3[0])
            p3_qpath(0, sets3[0], 0)
            for s in range(nseq):
                if s + 1 < nseq:
                    p3_pro(s + 1, sets3[(s + 1) % nset])
                    p3_qpath(s + 1, sets3[(s + 1) % nset], 0)
                p3_attn(s, sets3[s % nset], 0)
                while pend:
                    pend.pop()()
        else:
            p3_pro(0, sets3[0])
            for q0 in range(0, L, 512):
                p3_qpath(0, sets3[0], q0)
                p3_attn(0, sets3[0], q0)
            while pend:
                pend.pop()()
        P.pop()
        if MAXPH < 4:
            return

        def run_lockstep(gens):
            gens = list(gens)
            while gens:
                nxt = []
                for g_ in gens:
                    try:
                        next(g_)
                        nxt.append(g_)
                    except StopIteration:
                        pass
                gens = nxt

        nt = L // 128
        NTt = T // 128
        if ctx:
            chain_groups = [[(s, 0) for s in range(4)], [(s, 1) for s in range(4)]]
        else:
            chain_groups = [[(0, 0), (0, 1)]]
        nslot = len(chain_groups[0])
        skew = (nslot == 2)
        nslot_t = 2 * nslot if skew else nslot

        def run_chains(make_gen, nchain, skew):
            if not skew:
                for i in range(nt):
                    run_lockstep([make_gen(k, i, k) for k in range(nchain)])
                return

            def advance(active):
                parked = []
                while active:
                    nxt_ = []
                    for ent in active:
                        try:
                            r = next(ent[0])
                        except StopIteration:
                            continue
                        if ent[1] and r == "SPLIT":
                            parked.append(ent[0])
                        else:
                            nxt_.append(ent)
                    active = nxt_
                return parked
            parked = advance([[make_gen(k, 0, 2 * k), True] for k in range(nchain)])
            for i in range(nt):
                act_ = [[g_, False] for g_ in parked]
                if i + 1 < nt:
                    act_ += [[make_gen(k, i + 1, 2 * k + (i + 1) % 2), True] for k in range(nchain)]
                parked = advance(act_)

        P.push()
        EPS_T = P.sb("EPS", [128, 1])
        MEMSET(EPS_T, EPS_T[:, :], EPS)
        WGt = P.sb("WG", [33, 2, 128])
        LD(WGt, WGt[:, 0, :], wg_d[l, 0])
        LD(WGt, WGt[:, 1, :], wg_d[l, 1])
        QTt = P.sb("QT", [128, T])
        KTt = P.sb("KT", [128, T])
        GLR = P.sb("GLR", [33, T])
        GB = [P.sb("GB%d" % i, [128, T]) for i in range(2)]
        KV = P.sb("KV", [128, NTt, 384])
        OG = P.sb("OG", [64, 4, T])
        RN = lambda nm, shp: [P.sb("%s%d" % (nm, i), shp) for i in range(nslot_t)]
        RK = lambda nm, shp: [P.sb("%s%d" % (nm, i), shp) for i in range(nslot)]
        E1, LG, EB, EI, QD, KI, ER, KO = [RN(nm, [128, 128]) for nm in ("E1", "LG", "EB", "EI", "QD", "KI", "ER", "KO")]
        AM = RN("AM", [128, 512])
        QD4 = RN("QD4", [128, 512])
        Srings = [[P.sb("S%d_%d" % (k, i), [128, 64]) for i in range(3)] for k in range(nslot)]
        SQg = P.sb("SQg", [64, 512])
        RSg = P.sb("RSg", [64, 512])
        SGg = P.sb("SGg", [64, 512])
        OSg = [P.sb("OSg%d" % i, [64, 512]) for i in range(2)]
        load_fm(QTt, QTt[:, :], 5, 0, T)
        load_fm(KTt, KTt[:, :], 6, 0, T)
        MEMSET(GLR, GLR[:, :], 1.0)
        load_fm(GLR, GLR[0:32, :], 7, 0, T, 0, 32)
        for c in range(2):
            load_fm(GB[c], GB[c][:, :], 8 + c, 0, T)
        for gt in range(NTt):
            LD(KV, KV[:, gt, :], U_tm[gt * 128:(gt + 1) * 128, 128:512], dr=dU_tm)
        MEMSET(OG, OG[:, :, :], 0.0, eng="pool")

        def gla_body(s, d, tt_, k, kk, stt):
            gt = s * nt + tt_
            cs = slice(gt * 128, (gt + 1) * 128)
            TRI = TRIF if d == 0 else TRIB
            SM_ = SUF if d == 0 else SLB
            pz = bank()
            MM(pz, pz[:, 0:128], GLR, GLR[0:33, cs], WGt, WGt[0:33, d, :])
            ACT(E1[kk], E1[kk][:, :], pz, pz[:, 0:128], AF.Exp, scale=-1.0)
            ACT(LG[kk], LG[kk][:, :], E1[kk], E1[kk][:, :], AF.Ln, bias=1.0)
            yield
            pbt = bank()
            MM(pbt, pbt[:, 0:128], LG[kk], LG[kk][:, :], CST, CST[:, TRI, :])
            ACT(EB[kk], EB[kk][:, :], pbt, pbt[:, 0:128], AF.Exp, scale=-1.0 / 16)
            ACT(EI[kk], EI[kk][:, :], pbt, pbt[:, 0:128], AF.Exp, scale=1.0 / 16)
            STT(QD[kk], QD[kk][:, :], QTt, QTt[:, cs], 32.0 ** -0.5, EB[kk], EB[kk][:, :], ALU.mult, ALU.mult)
            TT(KI[kk], KI[kk][:, :], KTt, KTt[:, cs], EI[kk], EI[kk][:, :], ALU.mult)
            TT(QD4[kk], QD4[kk][:, :].rearrange("p (a b) -> p a b", a=4), QD[kk], QD[kk][:, :].unsqueeze(1).to_broadcast([128, 4, 128]),
               CST, CST[:, HM32, :].rearrange("p (a b) -> p a b", a=4)[:, :, 0:1].to_broadcast([128, 4, 128]), ALU.mult, eng="pool")
            yield
            pr = bank()
            MM(pr, pr[:, 0:128], CST, CST[:, SM_, :], LG[kk], LG[kk][:, :])
            ACT(ER[kk], ER[kk][:, :], pr, pr[:, 0:128], AF.Exp, scale=-1.0 / 16)
            TT(KO[kk], KO[kk][:, :], KV, KV[:, gt, 0:128], ER[kk], ER[kk][:, :], ALU.mult, eng="pool")
            yield
            pa = bank()
            MM(pa, pa[:, :], KI[kk], KI[kk][:, :], QD4[kk], QD4[kk][:, :])
            TT(AM[kk], AM[kk][:, :].rearrange("p (a b) -> p a b", a=4), pa, pa[:, :].rearrange("p (a b) -> p a b", a=4),
               CST, CST[:, TRI, :].unsqueeze(1).to_broadcast([128, 4, 128]), ALU.mult)
            yield
            po = bank_acc()
            for h in range(4):
                MM(po, po[0:64, h * 128:(h + 1) * 128], KV, KV[:, gt, 128 + 64 * h:128 + 64 * h + 64],
                   AM[kk], AM[kk][:, h * 128:(h + 1) * 128], st=(h == 0), sp=False, sg=True)
            yield "SPLIT"
            for ci, c in enumerate([0, 1] if d == 0 else [1, 0]):
                Scur = stt["S"]
                for h in range(4):
                    MM(po, po[0:64, h * 128 + 64 * c:h * 128 + 64 * c + 64], Scur, Scur[:, :],
                       QD4[kk], QD4[kk][:, h * 128 + 64 * c:h * 128 + 64 * c + 64], st=False, sp=(ci == 1 and h == 3), sg=True)
                pu = bank()
                MM(pu, pu[:, 0:256], KO[kk], KO[kk][64 * c:64 * c + 64, :], KV, KV[64 * c:64 * c + 64, gt, 128:384])
                stt["i"] += 1
                Sn = Srings[k][stt["i"] % 3]
                col = 64 * c + 63 if d == 0 else 64 * c
                for h in range(4):
                    STT(Sn, Sn[32 * h:32 * h + 32, :], Scur, Scur[32 * h:32 * h + 32, :], EB[kk][32 * h:32 * h + 32, col:col + 1],
                        pu, pu[32 * h:32 * h + 32, 64 * h:64 * h + 64], ALU.mult, ALU.add, rd=[EB[kk]])
                stt["S"] = Sn
                yield
            ogv = OG[:, :, cs]
            TT(OG, ogv, OG, ogv, po, po[0:64, :].rearrange("p (a b) -> p a b", a=4), ALU.add)

        for grp in chain_groups:
            states = []
            for k, (s, d) in enumerate(grp):
                S0 = Srings[k][0]
                if ctx:
                    MEMSET(S0, S0[:, :], 0.0)
                else:
                    LD(S0, S0[:, :], s_gla_d[l, d])
                states.append({"S": S0, "i": 0})
            run_chains(lambda k, i, kk: gla_body(grp[k][0], grp[k][1], (i if grp[k][1] == 0 else nt - 1 - i), k, kk, states[k]), len(grp), skew)
            if ctx:
                for k, (s, d) in enumerate(grp):
                    ST(o_gla[s, l, d], states[k]["S"], states[k]["S"][:, :], dw=dOUT)
        oi = 0
        for p0 in range(0, T, 512):
            n = 512
            for h in range(4):
                ACT(SQg, SQg[:, 0:n], OG, OG[:, h, p0:p0 + n], AF.Square)
                pb = bank()
                MM(pb, pb[0:64, 0:n], CST, CST[0:64, ONES, 0:64], SQg, SQg[:, 0:n])
                rstd_from(RSg, RSg[:, 0:n], pb, pb[0:64, 0:n], 64.0)
                c, base = h // 2, 64 * (h % 2)
                ACT(SGg, SGg[:, 0:n], GB[c], GB[c][base:base + 64, p0:p0 + n], AF.Silu)
                O_ = OSg[oi % 2]
                oi += 1
                STT(O_, O_[:, 0:n], OG, OG[:, h, p0:p0 + n], vf[0:64, 4:5], RSg, RSg[:, 0:n], ALU.mult, ALU.mult, rd=[vf])
                TT(O_, O_[:, 0:n], O_, O_[:, 0:n], SGg, SGg[:, 0:n], ALU.mult)
                ST(O_fm[256 + h * 64:256 + h * 64 + 64, p0:p0 + n], O_, O_[:, 0:n], dw=dO_fm)
        P.pop()
        if MAXPH < 5:
            return

        P.push()
        EPS_T = P.sb("EPS", [128, 1])
        MEMSET(EPS_T, EPS_T[:, :], EPS)
        XC = [P.sb("XC%d" % i, [128, T]) for i in range(4)]
        P.push()
        XPs = [P.sb("XP%d" % i, [128, L + 4]) for i in range(2)]
        for XP1 in XPs:
            MEMSET(XP1, XP1[:, 0:2], 0.0)
            MEMSET(XP1, XP1[:, L + 2:L + 4], 0.0)
        cw = CW[l]
        for g in range(4):
            for s in range(nseq):
                t0 = s * L
                xc = XC[g][:, t0:t0 + L]
                XP1 = XPs[(g * nseq + s) % 2]
                load_fm(XP1, XP1[:, 2:L + 2], 17 + g, t0, L)
                ce = "dve"
                TS(XC[g], xc, XP1, XP1[:, 0:L], cw[:, g, 0:1], ALU.mult, rd=[cw], eng=ce)
                for kk in range(1, 5):
                    STT(XC[g], xc, XP1, XP1[:, kk:kk + L], cw[:, g, kk:kk + 1], XC[g], xc, ALU.mult, ALU.add, rd=[cw], eng=ce)
                ACT(XC[g], xc, XC[g], xc, AF.Silu, bias=vf[:, 8 + g:9 + g], rd=[vf])
        P.pop()
        NF = 512
        ZT = [P.sb("ZT%d" % i, [128, NF]) for i in range(2)]
        XBT = P.sb("XBT", [128, NTt, 384])
        DTt = P.sb("DT", [128, NTt, 8])
        AA = P.sb("AA", [128, NTt, 8])
        YD = P.sb("YD", [64, 4, T])
        Hrings = [[P.sb("H%d_%d" % (k, i), [64, 256]) for i in range(3)] for k in range(nslot)]
        HT1 = RK("HT1", [64, 256])
        L4, E4, M4 = RK("L4", [128, 512]), RK("E4", [128, 512]), RK("M4", [128, 512])
        SMt, XCd, XD, AR, CS2 = RK("SMt", [128, 256]), RK("XCd", [128, 256]), RK("XD", [128, 256]), RK("AR", [128, 256]), RK("CS2", [128, 256])
        ECB, CD = RK("ECB", [64, 512]), RK("CD", [64, 512])
        DO = RK("DO", [128, 4])
        YZ = P.sb("YZ", [64, 4, NF])
        SQd = P.sb("SQd", [64, 4, NF])
        RSd = P.sb("RSd", [64, NF])
        SZ = P.sb("SZ", [64, NF])
        OSd = [P.sb("OSd%d" % i, [64, NF]) for i in range(2)]
        HIO = RK("HIO", [128, 2, 64])
        CS1 = P.sb("CS1", [64, T])
        DX = P.sb("DX", [64, NF])
        MEMSET(YD, YD[:, :, :], 0.0, eng="pool")
        for gt in range(NTt):
            LD(DTt, DTt[:, gt, :], U_tm[gt * 128:(gt + 1) * 128, 512:520], dr=dU_tm)
        CP(CS1, CS1[0:64, :], XC[3], XC[3][64:128, :], eng="act")
        CSG = [XC[3], CS1]
        for gt in range(NTt):
            pb = bank()
            for g in range(3):
                TR(pb, pb[:, g * 128:(g + 1) * 128], XC[g], XC[g][:, gt * 128:(gt + 1) * 128], IDN())
            CP(XBT, XBT[:, gt, :], pb, pb[:, 0:384], eng=("act" if gt % 2 else "dve"))
        TT(DTt, DTt[:, :, :], DTt, DTt[:, :, :], vt, vt[:, 192:200].unsqueeze(1).to_broadcast([128, NTt, 8]), ALU.add)
        ACT(DTt, DTt[:, :, :], DTt, DTt[:, :, :], AF.Exp)
        ACT(DTt, DTt[:, :, :], DTt, DTt[:, :, :], AF.Ln, bias=1.0)
        TT(AA, AA[:, :, :], DTt, DTt[:, :, :], NEGA[l], NEGA[l][:, :].unsqueeze(1).to_broadcast([128, NTt, 8]), ALU.mult)

        def ssd_body(s, d, tt_, k, kk, stt):
            gt = s * nt + tt_
            cs = slice(gt * 128, (gt + 1) * 128)
            TRI = TRIF if d == 0 else TRIB
            SM_ = SUF if d == 0 else SLB
            a4 = AA[:, gt, 4 * d:4 * d + 4]
            TT(L4[kk], L4[kk][:, :].rearrange("p (a b) -> p a b", a=4), CST, CST[:, SM_, :].unsqueeze(1).to_broadcast([128, 4, 128]),
               AA, a4.unsqueeze(2).to_broadcast([128, 4, 128]), ALU.mult, eng="pool")
            pe_ = bank()
            for h in range(4):
                MM(pe_, pe_[:, h * 128:(h + 1) * 128], L4[kk], L4[kk][:, h * 128:(h + 1) * 128], CST, CST[:, TRI, :])
            ACT(E4[kk], E4[kk][:, :], pe_, pe_[:, :], AF.Exp)
            yield
            TT(CS2[kk], CS2[kk][:, :].rearrange("p (a b) -> p a b", a=2), XC[3], XC[3][:, cs].unsqueeze(1).to_broadcast([128, 2, 128]),
               CST, CST[:, BO64, :].rearrange("p (a b) -> p a b", a=2)[:, :, 0:1].to_broadcast([128, 2, 128]), ALU.mult, eng="pool")
            pss = bank()
            MM(pss, pss[:, 0:256], XC[2], XC[2][:, cs], CS2[kk], CS2[kk][:, :])
            TT(SMt[kk], SMt[kk][:, :].rearrange("p (a b) -> p a b", a=2), pss, pss[:, 0:256].rearrange("p (a b) -> p a b", a=2),
               CST, CST[:, TRI, :].unsqueeze(1).to_broadcast([128, 2, 128]), ALU.mult)
            TT(M4[kk], M4[kk][:, :].rearrange("p (g r b) -> p g r b", g=2, r=2), E4[kk], E4[kk][:, :].rearrange("p (g r b) -> p g r b", g=2, r=2),
               SMt[kk], SMt[kk][:, :].rearrange("p (g b) -> p g b", g=2).unsqueeze(2).to_broadcast([128, 2, 2, 128]), ALU.mult)
            yield
            prs = bank()
            MM(prs, prs[:, 0:4], CST, CST[:, SM_, :], AA, a4)
            ACT(DO[kk], DO[kk][:, :], prs, prs[:, 0:4], AF.Exp)
            TT(XCd[kk], XCd[kk][:, :].rearrange("p (a b) -> p a b", a=4), XBT, XBT[:, gt, 0:256].rearrange("p (a b) -> p a b", a=4),
               DTt, DTt[:, gt, 4 * d:4 * d + 4].unsqueeze(2).to_broadcast([128, 4, 64]), ALU.mult, eng="pool")
            TT(XD[kk], XD[kk][:, :].rearrange("p (a b) -> p a b", a=4), XCd[kk], XCd[kk][:, :].rearrange("p (a b) -> p a b", a=4),
               DO[kk], DO[kk][:, :].unsqueeze(2).to_broadcast([128, 4, 64]), ALU.mult, eng="pool")
            yield
            CP(AR[kk], AR[kk][:, :].rearrange("p (a b) -> p a b", a=4), AA, a4.unsqueeze(2).to_broadcast([128, 4, 64]), eng="pool")
            pc = bank()
            for h in range(4):
                MM(pc, pc[0:64, h * 128:(h + 1) * 128], AR[kk], AR[kk][:, h * 64:(h + 1) * 64], CST, CST[:, TRI, :])
            ACT(ECB[kk], ECB[kk][:, :], pc, pc[0:64, :], AF.Exp)
            for g in range(2):
                TT(CD[kk], CD[kk][:, g * 256:(g + 1) * 256].rearrange("p (a b) -> p a b", a=2),
                   ECB[kk], ECB[kk][:, g * 256:(g + 1) * 256].rearrange("p (a b) -> p a b", a=2),
                   CSG[g], CSG[g][0:64, cs].unsqueeze(1).to_broadcast([64, 2, 128]), ALU.mult)
            yield
            py = bank_acc()
            for h in range(4):
                MM(py, py[0:64, h * 128:(h + 1) * 128], XCd[kk], XCd[kk][:, h * 64:(h + 1) * 64],
                   M4[kk], M4[kk][:, h * 128:(h + 1) * 128], st=(h == 0), sp=False, sg=True)
            yield "SPLIT"
            for ci, c in enumerate([0, 1] if d == 0 else [1, 0]):
                Hcur = stt["S"]
                for h in range(4):
                    MM(py, py[0:64, h * 128 + 64 * c:h * 128 + 64 * c + 64], Hcur, Hcur[:, h * 64:(h + 1) * 64],
                       CD[kk], CD[kk][:, h * 128 + 64 * c:h * 128 + 64 * c + 64], st=False, sp=(ci == 1 and h == 3), sg=True)
                ph = bank()
                for g in range(2):
                    MM(ph, ph[0:64, g * 128:(g + 1) * 128], XBT, XBT[64 * c:64 * c + 64, gt, 256 + 64 * g:256 + 64 * g + 64],
                       XD[kk], XD[kk][64 * c:64 * c + 64, g * 128:(g + 1) * 128])
                stt["i"] += 1
                Hn = Hrings[k][stt["i"] % 3]
                col = 64 * c + 63 if d == 0 else 64 * c
                TT(HT1[k], HT1[k][:, :].rearrange("p (a b) -> p a b", a=4), Hcur, Hcur[:, :].rearrange("p (a b) -> p a b", a=4),
                   ECB[kk], ECB[kk][:, :].rearrange("p (a b) -> p a b", a=4)[:, :, col:col + 1].to_broadcast([64, 4, 64]), ALU.mult, eng="pool")
                TT(Hn, Hn[:, :], HT1[k], HT1[k][:, :], ph, ph[0:64, 0:256], ALU.add)
                stt["S"] = Hn
                yield
            ydv = YD[:, :, cs]
            TT(YD, ydv, YD, ydv, py, py[0:64, :].rearrange("p (a b) -> p a b", a=4), ALU.add)

        for grp in chain_groups:
            states = []
            for k, (s, d) in enumerate(grp):
                H0 = Hrings[k][0]
                if ctx:
                    MEMSET(H0, H0[:, :], 0.0)
                else:
                    for c in range(2):
                        LD(HIO[k], HIO[k][:, c, :], s_ssd_d[l, d, c * 128:(c + 1) * 128, :])
                    pb = bank()
                    for c in range(2):
                        TR(pb, pb[0:64, c * 128:(c + 1) * 128], HIO[k], HIO[k][:, c, :], IDN())
                    CP(H0, H0[:, :], pb, pb[0:64, 0:256])
                states.append({"S": H0, "i": 0})
            run_chains(lambda k, i, kk: ssd_body(grp[k][0], grp[k][1], (i if grp[k][1] == 0 else nt - 1 - i), k, kk, states[k]), len(grp), False)
            if ctx:
                for k, (s, d) in enumerate(grp):
                    Hc = states[k]["S"]
                    pb = bank()
                    for c in range(2):
                        TR(pb, pb[:, c * 64:(c + 1) * 64], Hc, Hc[:, c * 128:(c + 1) * 128], IDN(64))
                    CP(HIO[k], HIO[k][:, :, :], pb, pb[:, 0:128].rearrange("p (a b) -> p a b", a=2))
                    for c in range(2):
                        ST(o_ssd[s, l, d, c * 128:(c + 1) * 128, :], HIO[k], HIO[k][:, c, :], dw=dOUT)
        oi = 0
        for p0 in range(0, T, NF):
            n = NF
            for c in range(2):
                load_fm(ZT[c], ZT[c][:, 0:n], 15 + c, p0, n)
            for h in range(4):
                c, base = h // 2, 64 * (h % 2)
                ACT(DX, DX[:, 0:n], XC[c], XC[c][base:base + 64, p0:p0 + n], AF.Identity, scale=vf[base:base + 64, 17 + c:18 + c], rd=[vf])
                TT(YZ, YZ[:, h, 0:n], DX, DX[:, 0:n], YD, YD[:, h, p0:p0 + n], ALU.add)
                ACT(SZ, SZ[:, 0:n], ZT[c], ZT[c][base:base + 64, 0:n], AF.Silu)
                TT(YZ, YZ[:, h, 0:n], YZ, YZ[:, h, 0:n], SZ, SZ[:, 0:n], ALU.mult)
            ACT(SQd, SQd[:, :, 0:n], YZ, YZ[:, :, 0:n], AF.Square)
            pb = bank()
            for h in range(4):
                MM(pb, pb[0:64, 0:n], CST, CST[0:64, ONES, 0:64], SQd, SQd[:, h, 0:n], st=(h == 0), sp=(h == 3))
            rstd_from(RSd, RSd[:, 0:n], pb, pb[0:64, 0:n], 256.0)
            for h in range(4):
                O_ = OSd[oi % 2]
                oi += 1
                STT(O_, O_[:, 0:n], YZ, YZ[:, h, 0:n], vf[0:64, 19 + h:20 + h], RSd, RSd[:, 0:n], ALU.mult, ALU.mult, rd=[vf])
                ST(O_fm[768 + h * 64:768 + h * 64 + 64, p0:p0 + n], O_, O_[:, 0:n], dw=dO_fm)
        P.pop()
        if MAXPH < 6:
            return

        P.push()
        EPS_T = P.sb("EPS", [128, 1])
        MEMSET(EPS_T, EPS_T[:, :], EPS)
        WO = P.sb("WO", [128, 8, 1024])
        WOh = P.sb("WOh", [128, 8, 1024], F32R)
        WOl = P.sb("WOl", [128, 8, 1024], F32R)
        for k in range(8):
            LD(WO, WO[:, k, :], wout_d[l][:, k, :])
        for k in range(8):
            CP(WOh, WOh[:, k, :], WO, WO[:, k, :])
            TT(WOl, WOl[:, k, :], WO, WO[:, k, :], WOh, WOh[:, k, :].bitcast(F32), ALU.subtract)
        OT = [P.sb("OT%d" % i, [128, 8, 128]) for i in range(3)]
        OTh = [P.sb("OTh%d" % i, [128, 8, 128], F32R) for i in range(2)]
        OTl = [P.sb("OTl%d" % i, [128, 8, 128], F32R) for i in range(2)]
        XR = [P.sb("XR%d" % i, [128, 1024]) for i in range(3)]
        YT = [P.sb("YT%d" % i, [128, 1024]) for i in range(3)]
        SQo = P.sb("SQo", [128, 1024])
        S6 = [P.sb("S6_%d" % i, [128, 2]) for i in range(3)]
        def prep6(t):
            LD(OT[t % 3], OT[t % 3][:, :, :], O_fm[:, t * 128:(t + 1) * 128].rearrange("(k p) t -> p k t", p=128), dr=dO_fm)
            LD(XR[t % 3], XR[t % 3][:, :], xsrc[t * 128:(t + 1) * 128, :], dr=dxs)
            CP(OTh[t % 2], OTh[t % 2][:, :, :], OT[t % 3], OT[t % 3][:, :, :])
            TT(OTl[t % 2], OTl[t % 2][:, :, :], OT[t % 3], OT[t % 3][:, :, :], OTh[t % 2], OTh[t % 2][:, :, :].bitcast(F32), ALU.subtract)
        prep6(0)
        for t in range(NT):
            i2 = t % 3
            j2 = t % 2
            if t + 1 < NT:
                prep6(t + 1)
            pbs = [bank(), bank()]
            for hf in range(2):
                n = 0
                for k in range(8):
                    for (x, y) in ((OTh[j2], WOh), (OTh[j2], WOl), (OTl[j2], WOh)):
                        MM(pbs[hf], pbs[hf][:, :], x, x[:, k, :], y, y[:, k, hf * 512:(hf + 1) * 512], st=(n == 0), sp=(n == 23))
                        n += 1
                ACT(SQo, SQo[:, hf * 512:(hf + 1) * 512], pbs[hf], pbs[hf][:, :], AF.Square)
            s6 = S6[i2]
            RSUM(s6, s6[:, 0:1], SQo, SQo[:, :])
            rstd_from(s6, s6[:, 1:2], s6, s6[:, 0:1], 1024.0)
            Y = YT[i2]
            for hf in range(2):
                STT(Y, Y[:, hf * 512:(hf + 1) * 512], pbs[hf], pbs[hf][:, :], s6[:, 1:2], GBC[l][who], GBC[l][who][:, hf * 512:(hf + 1) * 512],
                    ALU.mult, ALU.mult, rd=[s6])
            TT(Y, Y[:, :], Y, Y[:, :], XR[i2], XR[i2][:, :], ALU.add)
            ST(ydst[t * 128:(t + 1) * 128, :], Y, Y[:, :], dw=dyd)
        P.pop()

    maxl = int(os.environ.get("MK_MAXL", "2"))
    jobs = [int(j) for j in os.environ.get("MK_JOBS", "01")]
    for job in jobs:
        for l in range(maxl):
            run_layer(job, l)
    P.emit()
    es.close()
    return nc


def _rope_tables():
    def tab(R, n_tok=2048):
        nf = R // 4
        rows = n_tok // 64
        row = np.repeat(np.arange(rows), 64).astype(np.float32)
        col = np.tile(np.arange(64), rows).astype(np.float32)
        inv = (10000.0 ** (-np.arange(nf, dtype=np.float32) / nf)).astype(np.float32)
        ar = row[None, :] * inv[:, None]
        ac = col[None, :] * inv[:, None]
        cos = np.concatenate([np.cos(ar), np.cos(ar), np.cos(ac), np.cos(ac)], 0)
        sin = np.concatenate([-np.sin(ar), np.sin(ar), -np.sin(ac), np.sin(ac)], 0)
        return cos.astype(np.float32), sin.astype(np.float32)
    c64, s64 = tab(64)
    r64 = np.stack([np.concatenate([c64, c64], 0), np.concatenate([s64, s64], 0)], 0)
    c32, s32 = tab(32)
    r32 = np.zeros((2, 128, 2048), np.float32)
    r32[0, 64:96] = c32
    r32[1, 64:96] = s32
    return np.ascontiguousarray(r64), r32


def _consts():
    c = np.zeros((128, 10, 128), np.float32)
    i = np.arange(128)
    same = (i[:, None] // 64) == (i[None, :] // 64)
    c[:, 0, :] = np.eye(128)
    c[:, 1, :] = 1.0
    c[:, 2, :] = same & (i[:, None] <= i[None, :])
    c[:, 3, :] = same & (i[:, None] >= i[None, :])
    c[:, 4, :] = same & (i[:, None] > i[None, :])
    c[:, 5, :] = same & (i[:, None] < i[None, :])
    c[0, 6, :] = 1.0
    c[1, 7, :] = 1.0
    c[:, 8, :] = same
    c[:, 9, :] = (i[:, None] // 32) == (i[None, :] // 32)
    return c


_NC_CACHE = {}


def kernel(**inp):
    f = lambda k: np.asarray(inp[k], np.float32)
    x_prompt, x_sample = f("x_prompt"), f("x_sample")
    w_in, w_out, w_ada, b_ada = f("w_in"), f("w_out"), f("w_ada"), f("b_ada")
    shared = {}
    shared["wada_fm"] = np.stack([np.stack([_kpc(w_ada[l][:, g * 128:(g + 1) * 128]) for g in range(16)]) for l in range(2)])
    shared["wada_g"] = np.stack([_kpc(w_ada[l][:, 2048:3072]) for l in range(2)])
    shared["bada_fm"] = np.stack([np.ascontiguousarray(b_ada[l][:2048].reshape(16, 128).T) for l in range(2)])
    shared["bada_g"] = np.stack([np.stack([b_ada[l][2048:], b_ada[l][2048:]]) for l in range(2)])
    shared["npre"] = np.stack([np.ascontiguousarray(f("norm_pre")[l].reshape(8, 128).T) for l in range(2)])
    shared["npost"] = np.stack([np.stack([f("norm_post")[l]] * 2) for l in range(2)])
    shared["wfm"] = np.stack([np.stack([_kpc(_take_cols(w_in[l], g)) for g in FMG]) for l in range(2)])
    shared["wtm0"] = np.stack([_kpc(_take_cols(w_in[l], TM0)) for l in range(2)])
    shared["wtm1"] = np.stack([_kpc(_take_cols(w_in[l], TM1)) for l in range(2)])
    shared["wout"] = np.stack([_kpc(w_out[l]) for l in range(2)])
    shared["cst"] = _consts()
    r64, r32 = _rope_tables()
    shared["rope64"], shared["rope32"] = r64, r32
    vecfm = np.zeros((2, 128, 24), np.float32)
    vectm = np.zeros((2, 128, 384), np.float32)
    cw = np.zeros((2, 128, 4, 5), np.float32)
    sw64 = _swap(64)
    for l in range(2):
        gq, gk = f("gqa_q_norm")[l], f("gqa_k_norm")[l]
        vecfm[l, :, 0] = np.tile(gq, 2)
        vecfm[l, :, 1] = np.tile(gq[sw64], 2)
        vecfm[l, :, 2] = np.tile(gk, 2)
        vecfm[l, :, 3] = np.tile(gk[sw64], 2)
        vecfm[l, 0:64, 4] = f("gla_norm")[l]
        vecfm[l, :, 5] = f("mla_q_norm")[l][0:128]
        vecfm[l, 0:64, 6] = f("mla_q_norm")[l][128:192]
        vecfm[l, :, 7] = f("mla_kv_norm")[l]
        cb = f("ssd_conv_b")[l]
        cwl = f("ssd_conv_w")[l]
        for g in range(4):
            vecfm[l, :, 8 + g] = cb[g * 128:(g + 1) * 128]
            cw[l, :, g, :] = cwl[:, g * 128:(g + 1) * 128].T
        sd = f("ssd_d")[l]
        vecfm[l, :, 17] = np.repeat(sd[0:2], 64)
        vecfm[l, :, 18] = np.repeat(sd[2:4], 64)
        sn = f("ssd_norm")[l]
        for h in range(4):
            vecfm[l, 0:64, 19 + h] = sn[h * 64:(h + 1) * 64]
        vectm[l, :, 0:128] = f("mla_kv_norm")[l][None, :]
        vectm[l, :, 128:192] = gk[None, :]
        vectm[l, :, 192:200] = f("ssd_dt_bias")[l].reshape(8)[None, :]
        vectm[l, :, 200:208] = f("ssd_a_log")[l].reshape(8)[None, :]
    shared["vecfm"], shared["vectm"], shared["cw"] = vecfm, vectm, cw
    wg = np.zeros((2, 2, 33, 128), np.float32)
    for l in range(2):
        for d in range(2):
            wg[l, d, 16 * d:16 * d + 16, :] = f("gla_w_gate")[l, d]
            wg[l, d, 32, :] = f("gla_b_gate")[l, d]
    shared["wg"] = wg
    wuq = np.zeros((2, 2, 128, 2, 384), np.float32)
    sw32 = _swap(32)
    for l in range(2):
        w = f("mla_w_uq")[l]
        ws = w.copy()
        for h in range(4):
            ws[:, h * 96 + 64:h * 96 + 96] = w[:, h * 96 + 64 + sw32]
        for v, ww in enumerate([w, ws]):
            wuq[l, v, :, 0, :] = ww[0:128]
            wuq[l, v, 0:64, 1, :] = ww[128:192]
    shared["wuq"] = wuq
    shared["wukv"] = f("mla_w_ukv")
    c_ctx, c = f("c_ctx"), f("c")
    in_maps = []
    for core in range(8):
        b = core // 4
        m = dict(shared)
        m["x_p"] = np.ascontiguousarray(x_prompt[4 * core:4 * core + 4].reshape(1024, 1024))
        m["x_s"] = np.ascontiguousarray(x_sample[b])
        cf = np.stack([c_ctx, c[b]], -1)
        m["cfm"] = np.ascontiguousarray(cf.reshape(8, 128, 2).transpose(1, 0, 2))
        m["c_gk"] = np.ascontiguousarray(f("cache_gqa_k")[b].reshape(2, 512, 128))
        m["c_gv"] = np.ascontiguousarray(f("cache_gqa_v")[b].reshape(2, 512, 128))
        m["c_ckv"] = np.ascontiguousarray(f("cache_mla_ckv")[b])
        m["c_kr"] = np.ascontiguousarray(f("cache_mla_krope")[b])
        m["s_gla"] = np.ascontiguousarray(f("state_gla")[b].reshape(2, 2, 128, 64))
        m["s_ssd"] = np.ascontiguousarray(f("state_ssd")[b].reshape(2, 2, 256, 64))
        in_maps.append(m)
    if "nc" not in _NC_CACHE:
        _NC_CACHE["nc"] = build_program()
    res = run_bass_kernel_spmd(_NC_CACHE["nc"], in_maps, core_ids=list(range(8)))
    R = res.results
    if DEBUG:
        _NC_CACHE["last"] = R
    y_prompt = np.concatenate([R[c_]["y_p"].reshape(4, 256, 1024) for c_ in range(8)], 0)
    y_sample = np.stack([R[0]["y_s"], R[4]["y_s"]], 0)
    cat = lambda k: np.concatenate([R[c_][k] for c_ in range(8)], 0)
    new_gqa_k = cat("o_gk").reshape(32, 2, 256, 2, 64)
    new_gqa_v = cat("o_gv").reshape(32, 2, 256, 2, 64)
    new_ckv = cat("o_ckv")
    new_kr = cat("o_kr")
    new_gla = cat("o_gla").reshape(32, 2, 2, 4, 32, 64)
    new_ssd = cat("o_ssd").reshape(32, 2, 2, 4, 64, 64)
    return (y_prompt.astype(np.float32), y_sample.astype(np.float32), new_gqa_k, new_gqa_v, new_ckv, new_kr, new_gla, new_ssd)
```

```python
import os
import math
import numpy as np
from contextlib import ExitStack
import concourse.bass as bass
import concourse.mybir as mybir
from concourse.bass_utils import run_bass_kernel_spmd

F32 = mybir.dt.float32
F32R = mybir.dt.float32r
AF = mybir.ActivationFunctionType
ALU = mybir.AluOpType
AX = mybir.AxisListType
EPS = 1e-6
DEBUG = bool(int(os.environ.get("MK_DEBUG", "0")))
MAXPH = int(os.environ.get("MK_MAXPH", "99"))

OFF = {}
_sizes = [("gq", 256), ("gk", 128), ("gv", 128), ("ggate", 256), ("lq", 128), ("lk", 128), ("lv", 256), ("lglr", 32),
          ("lgate", 256), ("mcq", 192), ("mckv", 128), ("mkr", 32), ("mgate", 256), ("sz", 256), ("sx", 256),
          ("sB", 128), ("sC", 128), ("sdt", 8)]
_o = 0
for _n, _s in _sizes:
    OFF[_n] = _o
    _o += _s
assert _o == 2952


def _swap(R):
    nf = R // 4
    return np.concatenate([np.arange(nf, 2 * nf), np.arange(0, nf), np.arange(3 * nf, 4 * nf), np.arange(2 * nf, 3 * nf)])


def _fm_groups():
    r = np.arange
    g = []
    hq = lambda h: OFF["gq"] + 64 * h + r(64)
    hqs = lambda h: OFF["gq"] + 64 * h + _swap(64)
    g.append(np.concatenate([hq(0), hq(2)]))
    g.append(np.concatenate([hq(1), hq(3)]))
    g.append(OFF["gk"] + r(128))
    gg = lambda h: OFF["ggate"] + 64 * h + r(64)
    g.append(np.concatenate([gg(0), gg(2)]))
    g.append(np.concatenate([gg(1), gg(3)]))
    g.append(OFF["lq"] + r(128))
    g.append(OFF["lk"] + r(128))
    g.append(np.concatenate([OFF["lglr"] + r(32), -np.ones(96, int)]))
    g.append(OFF["lgate"] + r(128))
    g.append(OFF["lgate"] + 128 + r(128))
    g.append(OFF["mcq"] + r(128))
    g.append(np.concatenate([OFF["mcq"] + 128 + r(64), OFF["mkr"] + r(32), OFF["mkr"] + _swap(32)]))
    g.append(OFF["mckv"] + r(128))
    g.append(OFF["mgate"] + r(128))
    g.append(OFF["mgate"] + 128 + r(128))
    g.append(OFF["sz"] + r(128))
    g.append(OFF["sz"] + 128 + r(128))
    g.append(OFF["sx"] + r(128))
    g.append(OFF["sx"] + 128 + r(128))
    g.append(OFF["sB"] + r(128))
    g.append(OFF["sC"] + r(128))
    g.append(np.concatenate([hqs(0), hqs(2)]))
    g.append(np.concatenate([hqs(1), hqs(3)]))
    g.append(np.concatenate([OFF["gk"] + _swap(64), OFF["gk"] + 64 + _swap(64)]))
    return g


FMG = _fm_groups()
NG = len(FMG)
TM0 = np.concatenate([OFF["gv"] + np.arange(128), OFF["lk"] + np.arange(128), OFF["lv"] + np.arange(256)])
TM1 = np.concatenate([OFF["sdt"] + np.arange(8), OFF["mkr"] + np.arange(32), OFF["mckv"] + np.arange(128),
                      OFF["gk"] + np.arange(128)])
NTM1 = 296
UTW = 520


def _take_cols(w, idx):
    out = np.zeros((w.shape[0], len(idx)), np.float32)
    m = idx >= 0
    out[:, m] = w[:, idx[m]]
    return out


def _kpc(w):
    return np.ascontiguousarray(w.reshape(8, 128, w.shape[1]).transpose(1, 0, 2))


class Tk:
    __slots__ = ("t", "name", "w", "rs", "sem", "base", "dcnt", "kind")

    def __init__(self, t, name):
        self.t = t
        self.name = name
        self.w = None
        self.rs = []
        self.sem = None
        self.base = 0
        self.dcnt = 0
        self.kind = None

    def __getitem__(self, k):
        return self.t[k]


class Prog:
    ENG = ("pe", "act", "dve", "pool", "sp")
    CH = 30000

    def __init__(self, nc, es):
        self.nc = nc
        self.es = es
        self.ops = {e: [] for e in self.ENG}
        self.cnt = {e: 0 for e in self.ENG}
        self.esems = {e: [] for e in self.ENG}
        self.seen = {e: {} for e in self.ENG}
        self.pend = {e: [] for e in self.ENG}
        self.free_sems = {"sp": [], "pool": [], "act": []}
        self.scopes = []
        self.alltiles = []
        self.uid = 0
        self.eobj = {"pe": nc.tensor, "act": nc.scalar, "dve": nc.vector, "pool": nc.gpsimd, "sp": nc.sync}

    def sb(self, name, shape, dt=F32):
        self.uid += 1
        st = self.scopes[-1][0] if self.scopes else self.es
        t = st.enter_context(self.nc.sbuf_tensor("%s_%d" % (name, self.uid), list(shape), dt))
        tk = Tk(t, name)
        (self.scopes[-1][1] if self.scopes else self.alltiles).append(tk)
        return tk

    def ps(self, name, shape, dt=F32):
        t = self.es.enter_context(self.nc.psum_tensor(name, list(shape), dt))
        tk = Tk(t, name)
        self.alltiles.append(tk)
        return tk

    def dram(self, name):
        tk = Tk(None, name)
        return tk

    def _newsem(self, name):
        return self.es.enter_context(self.nc.semaphore(name)), 0

    def _esem(self, e, idx):
        k = idx // self.CH
        while len(self.esems[e]) <= k:
            self.esems[e].append(self._newsem("s_%s_%d" % (e, len(self.esems[e])))[0])
        return self.esems[e][k], idx % self.CH + 1

    def _tok_wait(self, tok):
        if tok[0] == "E":
            return self._esem(tok[1], tok[2])
        return tok[1], tok[2]

    def push(self):
        self.scopes.append((ExitStack(), []))

    def pop(self):
        st, tiles = self.scopes.pop()
        waits = []
        for e in self.ENG:
            if e != "sp" and self.cnt[e] > 0:
                waits.append(self._esem(e, self.cnt[e] - 1))
        for tk in tiles:
            if tk.sem is not None:
                waits.append((tk.sem, tk.base + 16 * tk.dcnt))
                self.free_sems[tk.kind].append((tk.sem, tk.base + 16 * tk.dcnt))
        for tk in self.alltiles:
            if tk.sem is not None:
                waits.append((tk.sem, tk.base + 16 * tk.dcnt))
        for e in self.ENG:
            self.pend[e].extend(waits)
        st.close()

    def op(self, eng, fn, reads=(), writes=(), dma=None):
        deps = []
        for r in reads:
            if r.w is not None:
                deps.append((r.w, True))
        for w in writes:
            if w.w is not None and not (dma is not None and w.w[0] == "D" and w.w[3] is dma):
                deps.append((w.w, False))
            for t in w.rs:
                deps.append((t, False))
        waits = {}

        def addw(sem, val):
            key = id(sem)
            if self.seen[eng].get(key, 0) >= val:
                return
            if key not in waits or waits[key][1] < val:
                waits[key] = (sem, val)

        for tok, raw in deps:
            if tok[0] == "E" and tok[1] == eng:
                if eng == "pe" or eng == "sp":
                    continue
            sem, val = self._tok_wait(tok)
            addw(sem, val)
        for sem, val in self.pend[eng]:
            addw(sem, val)
        self.pend[eng] = []
        for key, (sem, val) in waits.items():
            self.seen[eng][key] = val
        if dma is not None:
            assert dma.kind is None or dma.kind == eng, (dma.name, dma.kind, eng)
            if dma.sem is None:
                dma.kind = eng
                if self.free_sems[eng]:
                    dma.sem, dma.base = self.free_sems[eng].pop()
                else:
                    dma.sem, dma.base = self._newsem("d%d" % self.uid)
                self.uid += 1
            dma.dcnt += 1
            tok = ("D", dma.sem, dma.base + 16 * dma.dcnt, dma)
            inc = (dma.sem, 16)
        else:
            idx = self.cnt[eng]
            self.cnt[eng] += 1
            tok = ("E", eng, idx)
            sem, val = self._esem(eng, idx)
            inc = (sem, 1)
        eo = self.eobj[eng]
        for sem, val in waits.values():
            eo.wait_ge(sem, val)
        ins = fn(eo)
        ins.then_inc(inc[0], inc[1])
        for w in writes:
            w.w = tok
            w.rs = []
        for r in reads:
            if r not in writes:
                r.rs.append(tok)
        return tok

    def emit(self):
        fin = []
        for tk in self.alltiles:
            if tk.sem is not None:
                fin.append((tk.sem, tk.base + 16 * tk.dcnt))
        for lst in self.free_sems.values():
            for sem, val in lst:
                fin.append((sem, val))
        for e in self.ENG:
            if e != "sp" and self.cnt[e] > 0:
                fin.append(self._esem(e, self.cnt[e] - 1))
        for sem, val in fin:
            self.nc.sync.wait_ge(sem, val)


def build_program():
    nc = bass.Bass("TRN2", target_bir_lowering=False)
    es = ExitStack()
    P = Prog(nc, es)

    def din(name, shape):
        return nc.dram_tensor(name, list(shape), F32, kind="ExternalInput").ap()

    def dout(name, shape):
        return nc.dram_tensor(name, list(shape), F32, kind="ExternalOutput").ap()

    def dscr(name, shape):
        return nc.dram_tensor(name, list(shape), F32, kind="ExternalOutput" if DEBUG else "Internal").ap()

    x_p = din("x_p", [1024, 1024])
    x_s = din("x_s", [2048, 1024])
    cfm_d = din("cfm", [128, 8, 2])
    wada_fm_d = din("wada_fm", [2, 16, 128, 8, 128])
    wada_g_d = din("wada_g", [2, 128, 8, 1024])
    bada_fm_d = din("bada_fm", [2, 128, 16])
    bada_g_d = din("bada_g", [2, 2, 1024])
    npre_d = din("npre", [2, 128, 8])
    npost_d = din("npost", [2, 2, 1024])
    wfm_d = din("wfm", [2, NG, 128, 8, 128])
    wtm0_d = din("wtm0", [2, 128, 8, 512])
    wtm1_d = din("wtm1", [2, 128, 8, NTM1])
    wout_d = din("wout", [2, 128, 8, 1024])
    cst_d = din("cst", [128, 11, 128])
    rope64_d = din("rope64", [2, 128, 2048])
    rope32_d = din("rope32", [2, 128, 2048])
    vecfm_d = din("vecfm", [2, 128, 24])
    vectm_d = din("vectm", [2, 128, 384])
    wg_d = din("wg", [2, 2, 33, 128])
    wuq_d = din("wuq", [2, 2, 128, 2, 384])
    wukv_d = din("wukv", [2, 128, 512])
    c_gk_d = din("c_gk", [2, 512, 128])
    c_gv_d = din("c_gv", [2, 512, 128])
    c_ckv_d = din("c_ckv", [2, 512, 128])
    c_kr_d = din("c_kr", [2, 512, 32])
    s_gla_d = din("s_gla", [2, 2, 128, 64])
    s_ssd_d = din("s_ssd", [2, 2, 256, 64])
    y_p = dout("y_p", [1024, 1024])
    y_s = dout("y_s", [2048, 1024])
    o_gk = dout("o_gk", [4, 2, 256, 128])
    o_gv = dout("o_gv", [4, 2, 256, 128])
    o_ckv = dout("o_ckv", [4, 2, 256, 128])
    o_kr = dout("o_kr", [4, 2, 256, 32])
    o_gla = dout("o_gla", [4, 2, 2, 128, 64])
    o_ssd = dout("o_ssd", [4, 2, 2, 256, 64])
    U_fm = dscr("U_fm", [NG * 128, 2048])
    U_tm = dscr("U_tm", [2048, UTW])
    O_fm = dscr("O_fm", [1024, 2048])
    Y0 = dscr("Y0", [2048, 1024])
    dU_fm, dU_tm, dO_fm, dY0 = P.dram("U_fm"), P.dram("U_tm"), P.dram("O_fm"), P.dram("Y0")
    dOUT = P.dram("outs")

    def MM(o, o_ap, l, l_ap, r, r_ap, st=True, sp=True, tp=None, sg=False):
        kw = {}
        if tp is not None:
            kw["tile_position"] = tp
        if sg:
            kw["skip_group_check"] = True
        P.op("pe", lambda e: e.matmul(o_ap, lhsT=l_ap, rhs=r_ap, start=st, stop=sp, **kw), reads=[l, r], writes=[o])

    def TR(o, o_ap, a, a_ap, idn_ap):
        P.op("pe", lambda e: e.transpose(out=o_ap, in_=a_ap, identity=idn_ap), reads=[a, CST], writes=[o])

    def TT(o, o_ap, a, a_ap, b, b_ap, op, eng="dve"):
        P.op(eng, lambda e: e.tensor_tensor(out=o_ap, in0=a_ap, in1=b_ap, op=op), reads=[a, b], writes=[o])

    def TS(o, o_ap, a, a_ap, s1, op0, s2=None, op1=None, rd=(), eng="dve"):
        if op1 is None:
            P.op(eng, lambda e: e.tensor_scalar(out=o_ap, in0=a_ap, scalar1=s1, scalar2=None, op0=op0),
                 reads=[a] + list(rd), writes=[o])
        else:
            P.op(eng, lambda e: e.tensor_scalar(out=o_ap, in0=a_ap, scalar1=s1, scalar2=s2, op0=op0, op1=op1),
                 reads=[a] + list(rd), writes=[o])

    def STT(o, o_ap, a, a_ap, s, b, b_ap, op0, op1, rd=(), eng="dve"):
        P.op(eng, lambda e: e.scalar_tensor_tensor(out=o_ap, in0=a_ap, scalar=s, in1=b_ap, op0=op0, op1=op1),
             reads=[a, b] + list(rd), writes=[o])

    def ACT(o, o_ap, a, a_ap, func, bias=None, scale=None, rd=()):
        kw = {}
        if bias is not None:
            kw["bias"] = bias
        if scale is not None:
            kw["scale"] = scale
        P.op("act", lambda e: e.activation(out=o_ap, in_=a_ap, func=func, **kw), reads=[a] + list(rd), writes=[o])

    def CP(o, o_ap, a, a_ap, eng="dve"):
        if eng == "act":
            P.op("act", lambda e: e.copy(out=o_ap, in_=a_ap), reads=[a], writes=[o])
        else:
            P.op(eng, lambda e: e.tensor_copy(out=o_ap, in_=a_ap), reads=[a], writes=[o])

    def MEMSET(o, o_ap, val, eng="dve"):
        P.op(eng, lambda e: e.memset(o_ap, val), writes=[o])

    def RSUM(o, o_ap, a, a_ap):
        P.op("dve", lambda e: e.reduce_sum(out=o_ap, in_=a_ap, axis=AX.X), reads=[a], writes=[o])

    def RECIP(o, o_ap, a, a_ap):
        P.op("dve", lambda e: e.reciprocal(out=o_ap, in_=a_ap), reads=[a], writes=[o])

    def LD(tk, o_ap, src_ap, dr=None):
        P.op("sp", lambda e: e.dma_start(out=o_ap, in_=src_ap), reads=([dr] if dr is not None else []), writes=[tk], dma=tk)

    def ST(dst_ap, tk, i_ap, dw=None):
        P.op("pool", lambda e: e.dma_start(out=dst_ap, in_=i_ap), reads=[tk], writes=([dw] if dw is not None else []), dma=tk)

    PSB = [P.ps("psb%d" % i, [128, 512]) for i in range(8)]
    psi = [0]

    def bank():
        b = PSB[psi[0] % 4]
        psi[0] += 1
        return b
    pacc = [0]

    def bank_acc():
        b = PSB[4 + pacc[0] % 4]
        pacc[0] += 1
        return b

    CST = P.sb("CST", [128, 11, 128])
    LD(CST, CST[:, :, :], cst_d[:, :, :])
    IDN = lambda n=128: CST[0:n, 0, 0:n]
    ONES, TRIF, TRIB, SUF, SLB, BO64, HM32, PERM64 = 1, 2, 3, 4, 5, 8, 9, 10
    VFM = [P.sb("VFM%d" % l, [128, 24]) for l in range(2)]
    VTM = [P.sb("VTM%d" % l, [128, 384]) for l in range(2)]
    for l in range(2):
        LD(VFM[l], VFM[l][:, :], vecfm_d[l])
        LD(VTM[l], VTM[l][:, :], vectm_d[l])
    CW = [P.sb("CW%d" % l, [128, 4, 5]) for l in range(2)]
    cw_d = din("cw", [2, 128, 4, 5])
    for l in range(2):
        LD(CW[l], CW[l][:, :, :], cw_d[l])
    MODA = P.sb("MODA", [128, 2, 8, 2])
    MODB = P.sb("MODB", [128, 2, 8, 2])
    GBC = [[P.sb("GBC%d%d" % (l, w), [128, 1024]) for w in range(2)] for l in range(2)]
    NEGA = [P.sb("NEGA%d" % l, [128, 8]) for l in range(2)]

    P.push()
    CF = P.sb("CF", [128, 8, 2])
    LD(CF, CF[:, :, :], cfm_d[:, :, :])
    SC = P.sb("SC", [128, 8, 2])
    ACT(SC, SC[:, :, :], CF, CF[:, :, :], AF.Silu)
    for l in range(2):
        BF = P.sb("BF", [128, 16])
        LD(BF, BF[:, :], bada_fm_d[l])
        NP_ = P.sb("NP", [128, 8])
        LD(NP_, NP_[:, :], npre_d[l])
        MF = P.sb("MF", [128, 16, 2])
        WAs = [P.sb("WA%d" % i, [128, 8, 128]) for i in range(3)]
        for g in range(16):
            W = WAs[g % 3]
            LD(W, W[:, :, :], wada_fm_d[l, g])
            pb = bank()
            for k in range(8):
                MM(pb, pb[:, 0:2], W, W[:, k, :], SC, SC[:, k, :], st=(k == 0), sp=(k == 7))
            TS(MF, MF[:, g, :], pb, pb[:, 0:2], BF[:, g:g + 1], ALU.add, rd=[BF])
        for w in range(2):
            TS(MODA, MODA[:, l, :, w], MF, MF[:, 8:16, w], 1.0, ALU.add)
            TT(MODA, MODA[:, l, :, w], MODA, MODA[:, l, :, w], NP_, NP_[:, :], ALU.mult)
            CP(MODB, MODB[:, l, :, w], MF, MF[:, 0:8, w])
        WGt = P.sb("WGt", [128, 8, 1024])
        LD(WGt, WGt[:, :, :], wada_g_d[l])
        BG = P.sb("BG", [2, 1024])
        LD(BG, BG[:, :], bada_g_d[l])
        NPO = P.sb("NPO", [2, 1024])
        LD(NPO, NPO[:, :], npost_d[l])
        GN = P.sb("GN", [2, 1024])
        for hf in range(2):
            pb = bank()
            for k in range(8):
                MM(pb, pb[0:2, :], SC, SC[:, k, :], WGt, WGt[:, k, hf * 512:(hf + 1) * 512], st=(k == 0), sp=(k == 7))
            TT(GN, GN[:, hf * 512:(hf + 1) * 512], pb, pb[0:2, :], BG, BG[:, hf * 512:(hf + 1) * 512], ALU.add)
        TT(GN, GN[:, :], GN, GN[:, :], NPO, NPO[:, :], ALU.mult)
        for w in range(2):
            for hf in range(2):
                pb = bank()
                MM(pb, pb[:, :], CST, CST[0:2, 6 + w, :], GN, GN[0:2, hf * 512:(hf + 1) * 512])
                CP(GBC[l][w], GBC[l][w][:, hf * 512:(hf + 1) * 512], pb, pb[:, :], eng="act")
        ACT(NEGA[l], NEGA[l][:, :], VTM[l], VTM[l][:, 200:208], AF.Exp)
        TS(NEGA[l], NEGA[l][:, :], NEGA[l], NEGA[l][:, :], -1.0, ALU.mult)
    P.pop()

    def run_layer(job, l):
        ctx = (job == 0)
        who = 0 if ctx else 1
        T = 1024 if ctx else 2048
        nseq, L = (4, 256) if ctx else (1, 2048)
        NT = T // 128
        if l == 0:
            xsrc, dxs = (x_p if ctx else x_s), None
        else:
            xsrc, dxs = Y0, dY0
        if l == 1:
            ydst, dyd = (y_p if ctx else y_s), dOUT
        else:
            ydst, dyd = Y0, dY0
        koff = 0 if ctx else 512
        nk = koff + L
        nkt = nk // 128
        groups = list(range(21))
        vf, vt = VFM[l], VTM[l]

        def rstd_from(o, o_ap, a, a_ap, n):
            ACT(o, o_ap, a, a_ap, AF.Ln, bias=EPS_T[0:a_ap.shape[0], 0:1], scale=1.0 / n, rd=[EPS_T])
            ACT(o, o_ap, o, o_ap, AF.Exp, scale=-0.5)

        P.push()
        EPS_T = P.sb("EPS", [128, 1])
        MEMSET(EPS_T, EPS_T[:, :], EPS)
        TH = min(T, 1024)
        NTh = TH // 128
        HTh = P.sb("HTh", [128, 8, TH], F32R)
        HTl = P.sb("HTl", [128, 8, TH], F32R)
        HTt = [P.sb("HTt%d" % i, [128, 8, 128]) for i in range(2)]
        XT = [P.sb("XT%d" % i, [128, 1024]) for i in range(2)]
        SQ = P.sb("SQ", [128, 1024])
        ST1 = [P.sb("ST1_%d" % i, [128, 2]) for i in range(2)]
        nw1 = NTM1 if ctx else 8

        def split(hi, hi_ap, lo, lo_ap, src, src_ap):
            CP(hi, hi_ap, src, src_ap)
            TT(lo, lo_ap, src, src_ap, hi, hi_ap.bitcast(F32), ALU.subtract)

        def mm3(pb, o_ap, ah, al, a_sl, bh, bl, b_sl):
            n = 0
            for k in range(8):
                for (x, y) in ((ah, bh), (ah, bl), (al, bh)):
                    MM(pb, o_ap, x, a_sl(x, k), y, b_sl(y, k), st=(n == 0), sp=(n == 23))
                    n += 1

        W0h = P.sb("W0h", [128, 8, 512], F32R)
        W0l = P.sb("W0l", [128, 8, 512], F32R)
        W1h = P.sb("W1h", [128, 8, nw1], F32R)
        W1l = P.sb("W1l", [128, 8, nw1], F32R)
        TG = [P.sb("TG%d" % i, [128, 520]) for i in range(2)]
        TQ = [P.sb("TQ%d" % i, [128, 300]) for i in range(2)]

        def build_ht(half, tl):
            t = half * NTh + tl
            X = XT[t % 2]
            s1 = ST1[t % 2]
            Ht = HTt[t % 2]
            LD(X, X[:, :], xsrc[t * 128:(t + 1) * 128, :], dr=dxs)
            ACT(SQ, SQ[:, :], X, X[:, :], AF.Square)
            RSUM(s1, s1[:, 0:1], SQ, SQ[:, :])
            rstd_from(s1, s1[:, 1:2], s1, s1[:, 0:1], 1024.0)
            TS(X, X[:, :], X, X[:, :], s1[:, 1:2], ALU.mult, rd=[s1])
            for hf in range(2):
                pb = bank()
                for kk in range(4):
                    k = hf * 4 + kk
                    TR(pb, pb[:, kk * 128:(kk + 1) * 128], X, X[:, k * 128:(k + 1) * 128], IDN())
                for kk in range(4):
                    k = hf * 4 + kk
                    if kk % 2 == 0:
                        TS(Ht, Ht[:, k, :], pb, pb[:, kk * 128:(kk + 1) * 128],
                           MODA[:, l, k, who:who + 1], ALU.mult, MODB[:, l, k, who:who + 1], ALU.add, rd=[MODA, MODB])
                    else:
                        ACT(Ht, Ht[:, k, :], pb, pb[:, kk * 128:(kk + 1) * 128], AF.Identity,
                            bias=MODB[:, l, k, who:who + 1], scale=MODA[:, l, k, who:who + 1], rd=[MODA, MODB])
            cs_ = slice(tl * 128, (tl + 1) * 128)
            split(HTh, HTh[:, :, cs_], HTl, HTl[:, :, cs_], Ht, Ht[:, :, :])

        def tm_mm(half, tl):
            cs_ = slice(tl * 128, (tl + 1) * 128)
            pb = bank_acc()
            mm3(pb, pb[:, :], HTh, HTl, lambda x, k: x[:, k, cs_], W0h, W0l, lambda y, k: y[:, k, :])
            pb2 = bank_acc()
            mm3(pb2, pb2[:, 0:nw1], HTh, HTl, lambda x, k: x[:, k, cs_], W1h, W1l, lambda y, k: y[:, k, 0:nw1])
            return pb, pb2

        def tm_evac(half, tl, pbs_):
            pb, pb2 = pbs_
            t = half * NTh + tl
            G = TG[t % 2]
            Q = TQ[t % 2]
            CP(G, G[:, 0:512], pb, pb[:, :], eng="act")
            CP(G, G[:, 512:520], pb2, pb2[:, 0:8])
            ST(U_tm[t * 128:(t + 1) * 128, :], G, G[:, :], dw=dU_tm)
            if ctx:
                s, tt_ = t // 2, t % 2
                rows = slice(tt_ * 128, (tt_ + 1) * 128)
                ST(o_gv[s, l, rows, :], G, G[:, 0:128], dw=dOUT)
                CP(Q, Q[:, 0:32], pb2, pb2[:, 8:40])
                ST(o_kr[s, l, rows, :], Q, Q[:, 0:32], dw=dOUT)
                ACT(Q, Q[:, 32:160], pb2, pb2[:, 40:168], AF.Square)
                RSUM(Q, Q[:, 296:297], Q, Q[:, 32:160])
                rstd_from(Q, Q[:, 297:298], Q, Q[:, 296:297], 128.0)
                STT(Q, Q[:, 32:160], pb2, pb2[:, 40:168], Q[:, 297:298], vt, vt[:, 0:128], ALU.mult, ALU.mult)
                ST(o_ckv[s, l, rows, :], Q, Q[:, 32:160], dw=dOUT)
                ACT(Q, Q[:, 160:288], pb2, pb2[:, 168:296], AF.Square)
                RSUM(Q, Q[:, 298:300], Q, Q[:, 160:288].rearrange("p (a b) -> p a b", a=2))
                rstd_from(Q, Q[:, 298:300], Q, Q[:, 298:300], 64.0)
                TT(Q, Q[:, 160:288].rearrange("p (a b) -> p a b", a=2), pb2, pb2[:, 168:296].rearrange("p (a b) -> p a b", a=2),
                   Q, Q[:, 298:300].unsqueeze(2).to_broadcast([128, 2, 64]), ALU.mult)
                TT(Q, Q[:, 160:288].rearrange("p (a b) -> p a b", a=2), Q, Q[:, 160:288].rearrange("p (a b) -> p a b", a=2),
                   vt, vt[:, 128:192].unsqueeze(1).to_broadcast([128, 2, 64]), ALU.mult)
                ST(o_gk[s, l, rows, :], Q, Q[:, 160:288], dw=dOUT)

        P.push()
        WT0 = P.sb("WT0", [128, 8, 512])
        WT1 = P.sb("WT1", [128, 8, NTM1])
        LD(WT0, WT0[:, :, :], wtm0_d[l])
        LD(WT1, WT1[:, :, :], wtm1_d[l])
        build_ht(0, 0)
        split(W0h, W0h[:, :, :], W0l, W0l[:, :, :], WT0, WT0[:, :, :])
        split(W1h, W1h[:, :, 0:nw1], W1l, W1l[:, :, 0:nw1], WT1, WT1[:, :, 0:nw1])
        P.pop()
        WF = [P.sb("WF%d" % i, [128, 8, 128]) for i in range(2)]
        WFh = [P.sb("WFh%d" % i, [128, 8, 128], F32R) for i in range(2)]
        WFl = [P.sb("WFl%d" % i, [128, 8, 128], F32R) for i in range(2)]
        SG = [P.sb("SG%d" % i, [128, 512]) for i in range(3)]

        def prep_w(gi):
            W = WF[gi % 2]
            LD(W, W[:, :, :], wfm_d[l, groups[gi]])
            split(WFh[gi % 2], WFh[gi % 2][:, :, :], WFl[gi % 2], WFl[gi % 2][:, :, :], W, W[:, :, :])

        for half in range(T // TH):
            tok0 = half * TH
            prep_w(0)
            if True:
                if half > 0:
                    build_ht(half, 0)
                prev_ = None
                for tl in range(NTh):
                    if tl + 1 < NTh:
                        build_ht(half, tl + 1)
                    if prev_ is not None:
                        tm_evac(half, prev_[0], prev_[1])
                    prev_ = (tl, tm_mm(half, tl))
                tm_evac(half, prev_[0], prev_[1])
            si = 0
            for gi, g in enumerate(groups):
                if gi + 1 < len(groups):
                    prep_w(gi + 1)
                Wh_, Wl_ = WFh[gi % 2], WFl[gi % 2]
                for b in range(TH // 512):
                    bs_ = slice(b * 512, (b + 1) * 512)
                    pb = bank()
                    mm3(pb, pb[:, :], Wh_, Wl_, lambda x, k: x[:, k, :], HTh, HTl, lambda y, k, bs_=bs_: y[:, k, bs_])
                    S = SG[si % 3]
                    si += 1
                    CP(S, S[:, :], pb, pb[:, :], eng=("act" if si % 2 else "dve"))
                    ST(U_fm[g * 128:(g + 1) * 128, tok0 + b * 512:tok0 + (b + 1) * 512], S, S[:, :], dw=dU_fm)
        P.pop()
        if MAXPH < 2:
            return

        def attention(KT, kbase, kdim, QTt, qbase, qc0, nq, VAt, vcol, scale, gate_tk, gate_ap, orow, ocol0, PTs, misc, pend):
            OA = bank_acc()
            sb_ = [None] * nkt

            def S(kt):
                pb = bank()
                MM(pb, pb[:, 0:nq], KT, KT[kbase:kbase + kdim, kt * 128:(kt + 1) * 128],
                   QTt, QTt[qbase:qbase + kdim, qc0:qc0 + nq])
                sb_[kt] = pb
            S(0)
            if nkt > 1:
                S(1)
            for kt in range(nkt):
                if kt + 2 < nkt:
                    S(kt + 2)
                if kt == min(1, nkt - 1) and pend:
                    pend.pop()()
                PT = PTs[kt % len(PTs)]
                ACT(PT, PT[:, 0:nq], sb_[kt], sb_[kt][:, 0:nq], AF.Exp, scale=scale)
                MM(OA, OA[0:65, 0:nq], VAt, VAt[:, kt, vcol:vcol + 65], PT, PT[:, 0:nq], st=(kt == 0), sp=(kt == nkt - 1))
            mi = misc[0]
            misc[0] = (mi + 1) % 2
            RD, T1, OS = misc[1][mi]
            RECIP(RD, RD[64:65, 0:nq], OA, OA[64:65, 0:nq])

            def epi():
                pb = bank()
                MM(pb, pb[0:64, 0:nq], CST, CST[64:65, ONES, 0:64], RD, RD[64:65, 0:nq])
                TT(T1, T1[0:64, 0:nq], pb, pb[0:64, 0:nq], gate_tk, gate_ap, ALU.mult)
                TT(OS, OS[0:64, 0:nq], OA, OA[0:64, 0:nq], T1, T1[0:64, 0:nq], ALU.mult)
                ST(O_fm[orow:orow + 64, ocol0:ocol0 + nq], OS, OS[0:64, 0:nq], dw=dO_fm)
            pend.append(epi)

        def load_fm(tk, ap, g, c0, n, r0=0, r1=128):
            LD(tk, ap, U_fm[g * 128 + r0:g * 128 + r1, c0:c0 + n], dr=dU_fm)

        P.push()
        EPS_T = P.sb("EPS", [128, 1])
        MEMSET(EPS_T, EPS_T[:, :], EPS)
        if not ctx:
            ROPE = P.sb("ROPE", [128, 2, 2048])
            LD(ROPE, ROPE[:, 0, :], rope64_d[0])
            LD(ROPE, ROPE[:, 1, :], rope64_d[1])
        QT = [P.sb("QT%d" % i, [128, L]) for i in range(2)]
        QZ = [P.sb("QZ%d" % i, [128, L]) for i in range(4)]
        for h in range(4):
            zb = 64 * (1 - h // 2)
            MEMSET(QZ[h], QZ[h][zb:zb + 64, :], 0.0, eng="pool")
        KTt = P.sb("KT", [128, nk])
        GA = [P.sb("GA%d" % i, [128, L]) for i in range(2)]
        VA = P.sb("VA", [128, nkt, 130])
        XS_ = P.sb("XS", [128, min(L, 512)])
        TMPA = P.sb("TMPA", [128, min(L, 512)])
        TMPB = P.sb("TMPB", [128, min(L, 512)])
        RS = P.sb("RS", [128, min(L, 512)])
        PTs = [P.sb("PT%d" % i, [128, 512]) for i in range(3)]
        misc = [0, [(P.sb("RD%d" % i, [128, 512]), P.sb("T1%d" % i, [64, 512]), P.sb("OS%d" % i, [64, 512])) for i in range(2)]]
        pend = []
        CTM = P.sb("CTM", [128, 128])
        for s in range(nseq):
            t0 = s * L
            MEMSET(VA, VA[:, :, :], 1.0)
            for c in range(2):
                load_fm(QT[c], QT[c][:, :], c, t0, L)
                load_fm(GA[c], GA[c][:, :], 3 + c, t0, L)
                ACT(GA[c], GA[c][:, :], GA[c], GA[c][:, :], AF.Silu)
            load_fm(KTt, KTt[:, koff:koff + L], 2, t0, L)
            for kt in range(L // 128):
                LD(VA, VA[:, koff // 128 + kt, :].rearrange("p (a b) -> p a b", a=2)[:, :, 0:64],
                   U_tm[t0 + kt * 128:t0 + (kt + 1) * 128, 0:128].rearrange("p (a b) -> p a b", a=2), dr=dU_tm)
            if not ctx:
                bsel = 0
                for kt in range(4):
                    LD(VA, VA[:, kt, :].rearrange("p (a b) -> p a b", a=2)[:, :, 0:64],
                       c_gv_d[l, kt * 128:(kt + 1) * 128, :].rearrange("p (a b) -> p a b", a=2))
                    LD(CTM, CTM[:, :], c_gk_d[l, kt * 128:(kt + 1) * 128, :])
                    pb = bank()
                    TR(pb, pb[:, 0:128], CTM, CTM[:, :], IDN())
                    CP(KTt, KTt[:, kt * 128:(kt + 1) * 128], pb, pb[:, 0:128])
            for (tk, c0, gcol, gsw) in [(QT[0], 0, 0, 21), (QT[1], 0, 0, 22), (KTt, koff, 2, 23)]:
                for p0 in range(0, L, 512):
                    n = min(512, L - p0)
                    ap = tk[:, c0 + p0:c0 + p0 + n]
                    ACT(TMPA, TMPA[:, 0:n], tk, ap, AF.Square)
                    pb = bank()
                    MM(pb, pb[:, 0:n], CST, CST[:, BO64, :], TMPA, TMPA[:, 0:n])
                    rstd_from(RS, RS[:, 0:n], pb, pb[:, 0:n], 64.0)
                    if tk is KTt:
                        dsts = [(KTt, 0, 128, ap)]
                    else:
                        ci = 0 if tk is QT[0] else 1
                        dsts = [(QZ[ci], 0, 64, QZ[ci][0:64, p0:p0 + n]), (QZ[ci + 2], 64, 128, QZ[ci + 2][64:128, p0:p0 + n])]
                    if ctx:
                        for (dt_, r0, r1, dap) in dsts:
                            STT(dt_, dap, tk, tk[r0:r1, c0 + p0:c0 + p0 + n], vf[r0:r1, gcol:gcol + 1], RS, RS[r0:r1, 0:n], ALU.mult, ALU.mult, rd=[vf])
                    else:
                        pbx = bank()
                        MM(pbx, pbx[:, 0:n], CST, CST[:, PERM64, :], tk, ap)
                        STT(TMPA, TMPA[:, 0:n], tk, ap, vf[:, gcol:gcol + 1], ROPE, ROPE[:, 0, p0:p0 + n], ALU.mult, ALU.mult, rd=[vf])
                        STT(TMPB, TMPB[:, 0:n], pbx, pbx[:, 0:n], vf[:, gcol + 1:gcol + 2], ROPE, ROPE[:, 1, p0:p0 + n], ALU.mult, ALU.mult, rd=[vf])
                        TT(TMPA, TMPA[:, 0:n], TMPA, TMPA[:, 0:n], TMPB, TMPB[:, 0:n], ALU.add)
                        for (dt_, r0, r1, dap) in dsts:
                            TT(dt_, dap, TMPA, TMPA[r0:r1, 0:n], RS, RS[r0:r1, 0:n], ALU.mult)
            for h in range(4):
                c, base = h % 2, 64 * (h // 2)
                for q0 in range(0, L, 512):
                    nq = min(512, L - q0)
                    attention(KTt, 0, 128, QZ[h], 0, q0, nq, VA, (h // 2) * 65, 0.125,
                              GA[c], GA[c][base:base + 64, q0:q0 + nq], h * 64, t0 + q0, PTs, misc, pend)
            while pend:
                pend.pop()()
        P.pop()
        if MAXPH < 3:
            return

        P.push()
        EPS_T = P.sb("EPS", [128, 1])
        MEMSET(EPS_T, EPS_T[:, :], EPS)
        if not ctx:
            ROPE = P.sb("ROPE", [128, 2, 2048])
            LD(ROPE, ROPE[64:96, 0, :], rope32_d[0, 64:96, :])
            LD(ROPE, ROPE[64:96, 1, :], rope32_d[1, 64:96, :])
        WUQ = P.sb("WUQ", [128, 2, 384])
        LD(WUQ, WUQ[:, :, :], wuq_d[l, 0])
        WUQS = P.sb("WUQS", [128, 2, 384])
        LD(WUQS, WUQS[:, :, :], wuq_d[l, 1])
        WUKV = P.sb("WUKV", [128, 512])
        LD(WUKV, WUKV[:, :], wukv_d[l])
        WUV = P.sb("WUV", [128, 256])
        for h in range(4):
            CP(WUV, WUV[:, h * 64:(h + 1) * 64], WUKV, WUKV[:, h * 128 + 64:h * 128 + 128])
        CQA = P.sb("CQA", [128, L])
        G11 = P.sb("G11", [128, L])
        CKV = P.sb("CKV", [128, nk])
        GC = [P.sb("GC%d" % i, [128, L]) for i in range(2)]
        KM = [P.sb("KM%d" % i, [96, nk]) for i in range(4)]
        VC = P.sb("VC", [128, nkt, 260])
        QM = [P.sb("QM%d" % i, [96, 512]) for i in range(4)]
        TMPA = P.sb("TMPA", [128, 512])
        TMPB = P.sb("TMPB", [128, 512])
        RS = P.sb("RS", [128, 512])
        PTs = [P.sb("PT%d" % i, [128, 512]) for i in range(3)]
        misc = [0, [(P.sb("RD%d" % i, [128, 512]), P.sb("T1%d" % i, [64, 512]), P.sb("OS%d" % i, [64, 512])) for i in range(2)]]
        pend = []
        CTM = P.sb("CTM", [128, 160])
        for s in range(nseq):
            t0 = s * L
            MEMSET(VC, VC[:, :, :], 1.0)
            load_fm(CQA, CQA[:, :], 10, t0, L)
            load_fm(G11, G11[:, :], 11, t0, L)
            load_fm(CKV, CKV[:, koff:koff + L], 12, t0, L)
            for c in range(2):
                load_fm(GC[c], GC[c][:, :], 13 + c, t0, L)
                ACT(GC[c], GC[c][:, :], GC[c], GC[c][:, :], AF.Silu)
            if not ctx:
                for kt in range(4):
                    LD(CTM, CTM[:, 0:128], c_ckv_d[l, kt * 128:(kt + 1) * 128, :])
                    LD(CTM, CTM[:, 128:160], c_kr_d[l, kt * 128:(kt + 1) * 128, :])
                    pb = bank()
                    TR(pb, pb[:, 0:128], CTM, CTM[:, 0:128], IDN())
                    CP(CKV, CKV[:, kt * 128:(kt + 1) * 128], pb, pb[:, 0:128])
                    pb = bank()
                    TR(pb, pb[0:32, 0:128], CTM, CTM[:, 128:160], IDN())
                    for h in range(4):
                        CP(KM[h], KM[h][64:96, kt * 128:(kt + 1) * 128], pb, pb[0:32, 0:128], eng="dve")
            for p0 in range(0, L, 512):
                n = min(512, L - p0)
                ap = CKV[:, koff + p0:koff + p0 + n]
                ACT(TMPA, TMPA[:, 0:n], CKV, ap, AF.Square)
                pb = bank()
                MM(pb, pb[:, 0:n], CST, CST[:, ONES, :], TMPA, TMPA[:, 0:n])
                rstd_from(RS, RS[:, 0:n], pb, pb[:, 0:n], 128.0)
                STT(CKV, ap, CKV, ap, vf[:, 7:8], RS, RS[:, 0:n], ALU.mult, ALU.mult, rd=[vf])
                if ctx:
                    for h in range(4):
                        CP(KM[h], KM[h][64:96, koff + p0:koff + p0 + n], G11, G11[64:96, p0:p0 + n], eng=("act" if h % 2 else "dve"))
                else:
                    TT(TMPA, TMPA[64:96, 0:n], G11, G11[64:96, p0:p0 + n], ROPE, ROPE[64:96, 0, p0:p0 + n], ALU.mult)
                    load_fm(TMPB, TMPB[64:96, 0:n], 11, t0 + p0, n, 96, 128)
                    TT(TMPB, TMPB[64:96, 0:n], TMPB, TMPB[64:96, 0:n], ROPE, ROPE[64:96, 1, p0:p0 + n], ALU.mult)
                    for h in range(4):
                        TT(KM[h], KM[h][64:96, koff + p0:koff + p0 + n], TMPA, TMPA[64:96, 0:n], TMPB, TMPB[64:96, 0:n], ALU.add)
            for p0 in range(0, nk, 512):
                n = min(512, nk - p0)
                for h in range(4):
                    pb = bank()
                    MM(pb, pb[0:64, 0:n], WUKV, WUKV[:, h * 128:h * 128 + 64], CKV, CKV[:, p0:p0 + n])
                    CP(KM[h], KM[h][0:64, p0:p0 + n], pb, pb[0:64, 0:n], eng=("act" if h % 2 else "dve"))
            for kt in range(nkt):
                pb = bank()
                MM(pb, pb[:, 0:256], CKV, CKV[:, kt * 128:(kt + 1) * 128], WUV, WUV[:, :])
                CP(VC, VC[:, kt, :].rearrange("p (a b) -> p a b", a=4)[:, :, 0:64],
                   pb, pb[:, 0:256].rearrange("p (a b) -> p a b", a=4), eng=("act" if kt % 2 else "dve"))
            for q0 in range(0, L, 512):
                nq = min(512, L - q0)
                ACT(TMPA, TMPA[:, 0:nq], CQA, CQA[:, q0:q0 + nq], AF.Square)
                ACT(TMPB, TMPB[0:64, 0:nq], G11, G11[0:64, q0:q0 + nq], AF.Square)
                pb = bank()
                MM(pb, pb[:, 0:nq], CST, CST[:, ONES, :], TMPA, TMPA[:, 0:nq], st=True, sp=False)
                MM(pb, pb[:, 0:nq], CST, CST[0:64, ONES, :], TMPB, TMPB[0:64, 0:nq], st=False, sp=True)
                rstd_from(RS, RS[:, 0:nq], pb, pb[:, 0:nq], 192.0)
                STT(TMPA, TMPA[:, 0:nq], CQA, CQA[:, q0:q0 + nq], vf[:, 5:6], RS, RS[:, 0:nq], ALU.mult, ALU.mult, rd=[vf])
                STT(TMPB, TMPB[0:64, 0:nq], G11, G11[0:64, q0:q0 + nq], vf[0:64, 6:7], RS, RS[0:64, 0:nq], ALU.mult, ALU.mult, rd=[vf])
                for h in range(4):
                    pb = bank()
                    MM(pb, pb[0:96, 0:nq], WUQ, WUQ[:, 0, h * 96:(h + 1) * 96], TMPA, TMPA[:, 0:nq], st=True, sp=False)
                    MM(pb, pb[0:96, 0:nq], WUQ, WUQ[0:64, 1, h * 96:(h + 1) * 96], TMPB, TMPB[0:64, 0:nq], st=False, sp=True)
                    if ctx:
                        CP(QM[h], QM[h][0:96, 0:nq], pb, pb[0:96, 0:nq], eng="act")
                    else:
                        pb2 = bank()
                        MM(pb2, pb2[0:96, 0:nq], WUQS, WUQS[:, 0, h * 96:(h + 1) * 96], TMPA, TMPA[:, 0:nq], st=True, sp=False)
                        MM(pb2, pb2[0:96, 0:nq], WUQS, WUQS[0:64, 1, h * 96:(h + 1) * 96], TMPB, TMPB[0:64, 0:nq], st=False, sp=True)
                        CP(QM[h], QM[h][0:64, 0:nq], pb, pb[0:64, 0:nq], eng="act")
                        TT(QM[h], QM[h][64:96, 0:nq], pb, pb[64:96, 0:nq], ROPE, ROPE[64:96, 0, q0:q0 + nq], ALU.mult)
                        TT(RS, RS[64:96, 0:nq], pb2, pb2[64:96, 0:nq], ROPE, ROPE[64:96, 1, q0:q0 + nq], ALU.mult)
                        TT(QM[h], QM[h][64:96, 0:nq], QM[h], QM[h][64:96, 0:nq], RS, RS[64:96, 0:nq], ALU.add)
                for h in range(4):
                    c, base = h // 2, 64 * (h % 2)
                    attention(KM[h], 0, 96, QM[h], 0, 0, nq, VC, h * 65, 96.0 ** -0.5,
                              GC[c], GC[c][base:base + 64, q0:q0 + nq], 512 + h * 64, t0 + q0, PTs, misc, pend)
            while pend:
                pend.pop()()
        P.pop()
        if MAXPH < 4:
            return

        def run_lockstep(gens):
            gens = list(gens)
            while gens:
                nxt = []
                for g_ in gens:
                    try:
                        next(g_)
                        nxt.append(g_)
                    except StopIteration:
                        pass
                gens = nxt

        nt = L // 128
        NTt = T // 128
        if ctx:
            chain_groups = [[(s, 0) for s in range(4)], [(s, 1) for s in range(4)]]
        else:
            chain_groups = [[(0, 0), (0, 1)]]
        nslot = len(chain_groups[0])
        skew = (nslot == 2)
        nslot_t = 2 * nslot if skew else nslot

        def run_chains(make_gen, nchain, skew):
            if not skew:
                for i in range(nt):
                    run_lockstep([make_gen(k, i, k) for k in range(nchain)])
                return

            def advance(active):
                parked = []
                while active:
                    nxt_ = []
                    for ent in active:
                        try:
                            r = next(ent[0])
                        except StopIteration:
                            continue
                        if ent[1] and r == "SPLIT":
                            parked.append(ent[0])
                        else:
                            nxt_.append(ent)
                    active = nxt_
                return parked
            parked = advance([[make_gen(k, 0, 2 * k), True] for k in range(nchain)])
            for i in range(nt):
                act_ = [[g_, False] for g_ in parked]
                if i + 1 < nt:
                    act_ += [[make_gen(k, i + 1, 2 * k + (i + 1) % 2), True] for k in range(nchain)]
                parked = advance(act_)

        P.push()
        EPS_T = P.sb("EPS", [128, 1])
        MEMSET(EPS_T, EPS_T[:, :], EPS)
        WGt = P.sb("WG", [33, 2, 128])
        LD(WGt, WGt[:, 0, :], wg_d[l, 0])
        LD(WGt, WGt[:, 1, :], wg_d[l, 1])
        QTt = P.sb("QT", [128, T])
        KTt = P.sb("KT", [128, T])
        GLR = P.sb("GLR", [33, T])
        GB = [P.sb("GB%d" % i, [128, T]) for i in range(2)]
        KV = P.sb("KV", [128, NTt, 384])
        OG = P.sb("OG", [64, 4, T])
        RN = lambda nm, shp: [P.sb("%s%d" % (nm, i), shp) for i in range(nslot_t)]
        RK = lambda nm, shp: [P.sb("%s%d" % (nm, i), shp) for i in range(nslot)]
        E1, LG, EB, EI, QD, KI, ER, KO = [RN(nm, [128, 128]) for nm in ("E1", "LG", "EB", "EI", "QD", "KI", "ER", "KO")]
        AM = RN("AM", [128, 512])
        QD4 = RN("QD4", [128, 512])
        Srings = [[P.sb("S%d_%d" % (k, i), [128, 64]) for i in range(3)] for k in range(nslot)]
        SQg = P.sb("SQg", [64, 512])
        RSg = P.sb("RSg", [64, 512])
        SGg = P.sb("SGg", [64, 512])
        OSg = [P.sb("OSg%d" % i, [64, 512]) for i in range(2)]
        load_fm(QTt, QTt[:, :], 5, 0, T)
        load_fm(KTt, KTt[:, :], 6, 0, T)
        MEMSET(GLR, GLR[:, :], 1.0)
        load_fm(GLR, GLR[0:32, :], 7, 0, T, 0, 32)
        for c in range(2):
            load_fm(GB[c], GB[c][:, :], 8 + c, 0, T)
        for gt in range(NTt):
            LD(KV, KV[:, gt, :], U_tm[gt * 128:(gt + 1) * 128, 128:512], dr=dU_tm)
        MEMSET(OG, OG[:, :, :], 0.0, eng="pool")

        def gla_body(s, d, tt_, k, kk, stt):
            gt = s * nt + tt_
            cs = slice(gt * 128, (gt + 1) * 128)
            TRI = TRIF if d == 0 else TRIB
            SM_ = SUF if d == 0 else SLB
            pz = bank()
            MM(pz, pz[:, 0:128], GLR, GLR[0:33, cs], WGt, WGt[0:33, d, :])
            ACT(E1[kk], E1[kk][:, :], pz, pz[:, 0:128], AF.Exp, scale=-1.0)
            ACT(LG[kk], LG[kk][:, :], E1[kk], E1[kk][:, :], AF.Ln, bias=1.0)
            yield
            pbt = bank()
            MM(pbt, pbt[:, 0:128], LG[kk], LG[kk][:, :], CST, CST[:, TRI, :])
            ACT(EB[kk], EB[kk][:, :], pbt, pbt[:, 0:128], AF.Exp, scale=-1.0 / 16)
            ACT(EI[kk], EI[kk][:, :], pbt, pbt[:, 0:128], AF.Exp, scale=1.0 / 16)
            STT(QD[kk], QD[kk][:, :], QTt, QTt[:, cs], 32.0 ** -0.5, EB[kk], EB[kk][:, :], ALU.mult, ALU.mult)
            TT(KI[kk], KI[kk][:, :], KTt, KTt[:, cs], EI[kk], EI[kk][:, :], ALU.mult)
            TT(QD4[kk], QD4[kk][:, :].rearrange("p (a b) -> p a b", a=4), QD[kk], QD[kk][:, :].unsqueeze(1).to_broadcast([128, 4, 128]),
               CST, CST[:, HM32, :].rearrange("p (a b) -> p a b", a=4)[:, :, 0:1].to_broadcast([128, 4, 128]), ALU.mult, eng="pool")
            yield
            pr = bank()
            MM(pr, pr[:, 0:128], CST, CST[:, SM_, :], LG[kk], LG[kk][:, :])
            ACT(ER[kk], ER[kk][:, :], pr, pr[:, 0:128], AF.Exp, scale=-1.0 / 16)
            TT(KO[kk], KO[kk][:, :], KV, KV[:, gt, 0:128], ER[kk], ER[kk][:, :], ALU.mult, eng="pool")
            yield
            pa = bank()
            MM(pa, pa[:, :], KI[kk], KI[kk][:, :], QD4[kk], QD4[kk][:, :])
            TT(AM[kk], AM[kk][:, :].rearrange("p (a b) -> p a b", a=4), pa, pa[:, :].rearrange("p (a b) -> p a b", a=4),
               CST, CST[:, TRI, :].unsqueeze(1).to_broadcast([128, 4, 128]), ALU.mult)
            yield
            po = bank_acc()
            for h in range(4):
                MM(po, po[0:64, h * 128:(h + 1) * 128], KV, KV[:, gt, 128 + 64 * h:128 + 64 * h + 64],
                   AM[kk], AM[kk][:, h * 128:(h + 1) * 128], st=(h == 0), sp=False, sg=True)
            yield "SPLIT"
            for ci, c in enumerate([0, 1] if d == 0 else [1, 0]):
                Scur = stt["S"]
                for h in range(4):
                    MM(po, po[0:64, h * 128 + 64 * c:h * 128 + 64 * c + 64], Scur, Scur[:, :],
                       QD4[kk], QD4[kk][:, h * 128 + 64 * c:h * 128 + 64 * c + 64], st=False, sp=(ci == 1 and h == 3), sg=True)
                pu = bank()
                MM(pu, pu[:, 0:256], KO[kk], KO[kk][64 * c:64 * c + 64, :], KV, KV[64 * c:64 * c + 64, gt, 128:384])
                stt["i"] += 1
                Sn = Srings[k][stt["i"] % 3]
                col = 64 * c + 63 if d == 0 else 64 * c
                for h in range(4):
                    STT(Sn, Sn[32 * h:32 * h + 32, :], Scur, Scur[32 * h:32 * h + 32, :], EB[kk][32 * h:32 * h + 32, col:col + 1],
                        pu, pu[32 * h:32 * h + 32, 64 * h:64 * h + 64], ALU.mult, ALU.add, rd=[EB[kk]])
                stt["S"] = Sn
                yield
            ogv = OG[:, :, cs]
            TT(OG, ogv, OG, ogv, po, po[0:64, :].rearrange("p (a b) -> p a b", a=4), ALU.add)

        for grp in chain_groups:
            states = []
            for k, (s, d) in enumerate(grp):
                S0 = Srings[k][0]
                if ctx:
                    MEMSET(S0, S0[:, :], 0.0)
                else:
                    LD(S0, S0[:, :], s_gla_d[l, d])
                states.append({"S": S0, "i": 0})
            run_chains(lambda k, i, kk: gla_body(grp[k][0], grp[k][1], (i if grp[k][1] == 0 else nt - 1 - i), k, kk, states[k]), len(grp), skew)
            if ctx:
                for k, (s, d) in enumerate(grp):
                    ST(o_gla[s, l, d], states[k]["S"], states[k]["S"][:, :], dw=dOUT)
        oi = 0
        for p0 in range(0, T, 512):
            n = 512
            for h in range(4):
                ACT(SQg, SQg[:, 0:n], OG, OG[:, h, p0:p0 + n], AF.Square)
                pb = bank()
                MM(pb, pb[0:64, 0:n], CST, CST[0:64, ONES, 0:64], SQg, SQg[:, 0:n])
                rstd_from(RSg, RSg[:, 0:n], pb, pb[0:64, 0:n], 64.0)
                c, base = h // 2, 64 * (h % 2)
                ACT(SGg, SGg[:, 0:n], GB[c], GB[c][base:base + 64, p0:p0 + n], AF.Silu)
                O_ = OSg[oi % 2]
                oi += 1
                STT(O_, O_[:, 0:n], OG, OG[:, h, p0:p0 + n], vf[0:64, 4:5], RSg, RSg[:, 0:n], ALU.mult, ALU.mult, rd=[vf])
                TT(O_, O_[:, 0:n], O_, O_[:, 0:n], SGg, SGg[:, 0:n], ALU.mult)
                ST(O_fm[256 + h * 64:256 + h * 64 + 64, p0:p0 + n], O_, O_[:, 0:n], dw=dO_fm)
        P.pop()
        if MAXPH < 5:
            return

        P.push()
        EPS_T = P.sb("EPS", [128, 1])
        MEMSET(EPS_T, EPS_T[:, :], EPS)
        XC = [P.sb("XC%d" % i, [128, T]) for i in range(4)]
        P.push()
        XPs = [P.sb("XP%d" % i, [128, L + 4]) for i in range(2)]
        for XP1 in XPs:
            MEMSET(XP1, XP1[:, 0:2], 0.0)
            MEMSET(XP1, XP1[:, L + 2:L + 4], 0.0)
        cw = CW[l]
        for g in range(4):
            for s in range(nseq):
                t0 = s * L
                xc = XC[g][:, t0:t0 + L]
                XP1 = XPs[(g * nseq + s) % 2]
                load_fm(XP1, XP1[:, 2:L + 2], 17 + g, t0, L)
                ce = "dve"
                TS(XC[g], xc, XP1, XP1[:, 0:L], cw[:, g, 0:1], ALU.mult, rd=[cw], eng=ce)
                for kk in range(1, 5):
                    STT(XC[g], xc, XP1, XP1[:, kk:kk + L], cw[:, g, kk:kk + 1], XC[g], xc, ALU.mult, ALU.add, rd=[cw], eng=ce)
                ACT(XC[g], xc, XC[g], xc, AF.Silu, bias=vf[:, 8 + g:9 + g], rd=[vf])
        P.pop()
        NF = 512
        ZT = [P.sb("ZT%d" % i, [128, NF]) for i in range(2)]
        XBT = P.sb("XBT", [128, NTt, 384])
        DTt = P.sb("DT", [128, NTt, 8])
        AA = P.sb("AA", [128, NTt, 8])
        YD = P.sb("YD", [64, 4, T])
        Hrings = [[P.sb("H%d_%d" % (k, i), [64, 256]) for i in range(3)] for k in range(nslot)]
        HT1 = RK("HT1", [64, 256])
        L4, E4, M4 = RK("L4", [128, 512]), RK("E4", [128, 512]), RK("M4", [128, 512])
        SMt, XCd, XD, AR, CS2 = RK("SMt", [128, 256]), RK("XCd", [128, 256]), RK("XD", [128, 256]), RK("AR", [128, 256]), RK("CS2", [128, 256])
        ECB, CD = RK("ECB", [64, 512]), RK("CD", [64, 512])
        DO = RK("DO", [128, 4])
        YZ = P.sb("YZ", [64, 4, NF])
        SQd = P.sb("SQd", [64, 4, NF])
        RSd = P.sb("RSd", [64, NF])
        SZ = P.sb("SZ", [64, NF])
        OSd = [P.sb("OSd%d" % i, [64, NF]) for i in range(2)]
        HIO = RK("HIO", [128, 2, 64])
        CS1 = P.sb("CS1", [64, T])
        DX = P.sb("DX", [64, NF])
        MEMSET(YD, YD[:, :, :], 0.0, eng="pool")
        for gt in range(NTt):
            LD(DTt, DTt[:, gt, :], U_tm[gt * 128:(gt + 1) * 128, 512:520], dr=dU_tm)
        CP(CS1, CS1[0:64, :], XC[3], XC[3][64:128, :], eng="act")
        CSG = [XC[3], CS1]
        for gt in range(NTt):
            pb = bank()
            for g in range(3):
                TR(pb, pb[:, g * 128:(g + 1) * 128], XC[g], XC[g][:, gt * 128:(gt + 1) * 128], IDN())
            CP(XBT, XBT[:, gt, :], pb, pb[:, 0:384], eng=("act" if gt % 2 else "dve"))
        TT(DTt, DTt[:, :, :], DTt, DTt[:, :, :], vt, vt[:, 192:200].unsqueeze(1).to_broadcast([128, NTt, 8]), ALU.add)
        ACT(DTt, DTt[:, :, :], DTt, DTt[:, :, :], AF.Exp)
        ACT(DTt, DTt[:, :, :], DTt, DTt[:, :, :], AF.Ln, bias=1.0)
        TT(AA, AA[:, :, :], DTt, DTt[:, :, :], NEGA[l], NEGA[l][:, :].unsqueeze(1).to_broadcast([128, NTt, 8]), ALU.mult)

        def ssd_body(s, d, tt_, k, kk, stt):
            gt = s * nt + tt_
            cs = slice(gt * 128, (gt + 1) * 128)
            TRI = TRIF if d == 0 else TRIB
            SM_ = SUF if d == 0 else SLB
            a4 = AA[:, gt, 4 * d:4 * d + 4]
            TT(L4[kk], L4[kk][:, :].rearrange("p (a b) -> p a b", a=4), CST, CST[:, SM_, :].unsqueeze(1).to_broadcast([128, 4, 128]),
               AA, a4.unsqueeze(2).to_broadcast([128, 4, 128]), ALU.mult, eng="pool")
            pe_ = bank()
            for h in range(4):
                MM(pe_, pe_[:, h * 128:(h + 1) * 128], L4[kk], L4[kk][:, h * 128:(h + 1) * 128], CST, CST[:, TRI, :])
            ACT(E4[kk], E4[kk][:, :], pe_, pe_[:, :], AF.Exp)
            yield
            TT(CS2[kk], CS2[kk][:, :].rearrange("p (a b) -> p a b", a=2), XC[3], XC[3][:, cs].unsqueeze(1).to_broadcast([128, 2, 128]),
               CST, CST[:, BO64, :].rearrange("p (a b) -> p a b", a=2)[:, :, 0:1].to_broadcast([128, 2, 128]), ALU.mult, eng="pool")
            pss = bank()
            MM(pss, pss[:, 0:256], XC[2], XC[2][:, cs], CS2[kk], CS2[kk][:, :])
            TT(SMt[kk], SMt[kk][:, :].rearrange("p (a b) -> p a b", a=2), pss, pss[:, 0:256].rearrange("p (a b) -> p a b", a=2),
               CST, CST[:, TRI, :].unsqueeze(1).to_broadcast([128, 2, 128]), ALU.mult)
            TT(M4[kk], M4[kk][:, :].rearrange("p (g r b) -> p g r b", g=2, r=2), E4[kk], E4[kk][:, :].rearrange("p (g r b) -> p g r b", g=2, r=2),
               SMt[kk], SMt[kk][:, :].rearrange("p (g b) -> p g b", g=2).unsqueeze(2).to_broadcast([128, 2, 2, 128]), ALU.mult)
            yield
            prs = bank()
            MM(prs, prs[:, 0:4], CST, CST[:, SM_, :], AA, a4)
            ACT(DO[kk], DO[kk][:, :], prs, prs[:, 0:4], AF.Exp)
            TT(XCd[kk], XCd[kk][:, :].rearrange("p (a b) -> p a b", a=4), XBT, XBT[:, gt, 0:256].rearrange("p (a b) -> p a b", a=4),
               DTt, DTt[:, gt, 4 * d:4 * d + 4].unsqueeze(2).to_broadcast([128, 4, 64]), ALU.mult, eng="pool")
            TT(XD[kk], XD[kk][:, :].rearrange("p (a b) -> p a b", a=4), XCd[kk], XCd[kk][:, :].rearrange("p (a b) -> p a b", a=4),
               DO[kk], DO[kk][:, :].unsqueeze(2).to_broadcast([128, 4, 64]), ALU.mult, eng="pool")
            yield
            CP(AR[kk], AR[kk][:, :].rearrange("p (a b) -> p a b", a=4), AA, a4.unsqueeze(2).to_broadcast([128, 4, 64]), eng="pool")
            pc = bank()
            for h in range(4):
                MM(pc, pc[0:64, h * 128:(h + 1) * 128], AR[kk], AR[kk][:, h * 64:(h + 1) * 64], CST, CST[:, TRI, :])
            ACT(ECB[kk], ECB[kk][:, :], pc, pc[0:64, :], AF.Exp)
            for g in range(2):
                TT(CD[kk], CD[kk][:, g * 256:(g + 1) * 256].rearrange("p (a b) -> p a b", a=2),
                   ECB[kk], ECB[kk][:, g * 256:(g + 1) * 256].rearrange("p (a b) -> p a b", a=2),
                   CSG[g], CSG[g][0:64, cs].unsqueeze(1).to_broadcast([64, 2, 128]), ALU.mult)
            yield
            py = bank_acc()
            for h in range(4):
                MM(py, py[0:64, h * 128:(h + 1) * 128], XCd[kk], XCd[kk][:, h * 64:(h + 1) * 64],
                   M4[kk], M4[kk][:, h * 128:(h + 1) * 128], st=(h == 0), sp=False, sg=True)
            yield "SPLIT"
            for ci, c in enumerate([0, 1] if d == 0 else [1, 0]):
                Hcur = stt["S"]
                for h in range(4):
                    MM(py, py[0:64, h * 128 + 64 * c:h * 128 + 64 * c + 64], Hcur, Hcur[:, h * 64:(h + 1) * 64],
                       CD[kk], CD[kk][:, h * 128 + 64 * c:h * 128 + 64 * c + 64], st=False, sp=(ci == 1 and h == 3), sg=True)
                ph = bank()
                for g in range(2):
                    MM(ph, ph[0:64, g * 128:(g + 1) * 128], XBT, XBT[64 * c:64 * c + 64, gt, 256 + 64 * g:256 + 64 * g + 64],
                       XD[kk], XD[kk][64 * c:64 * c + 64, g * 128:(g + 1) * 128])
                stt["i"] += 1
                Hn = Hrings[k][stt["i"] % 3]
                col = 64 * c + 63 if d == 0 else 64 * c
                TT(HT1[k], HT1[k][:, :].rearrange("p (a b) -> p a b", a=4), Hcur, Hcur[:, :].rearrange("p (a b) -> p a b", a=4),
                   ECB[kk], ECB[kk][:, :].rearrange("p (a b) -> p a b", a=4)[:, :, col:col + 1].to_broadcast([64, 4, 64]), ALU.mult, eng="pool")
                TT(Hn, Hn[:, :], HT1[k], HT1[k][:, :], ph, ph[0:64, 0:256], ALU.add)
                stt["S"] = Hn
                yield
            ydv = YD[:, :, cs]
            TT(YD, ydv, YD, ydv, py, py[0:64, :].rearrange("p (a b) -> p a b", a=4), ALU.add)

        for grp in chain_groups:
            states = []
            for k, (s, d) in enumerate(grp):
                H0 = Hrings[k][0]
                if ctx:
                    MEMSET(H0, H0[:, :], 0.0)
                else:
                    for c in range(2):
                        LD(HIO[k], HIO[k][:, c, :], s_ssd_d[l, d, c * 128:(c + 1) * 128, :])
                    pb = bank()
                    for c in range(2):
                        TR(pb, pb[0:64, c * 128:(c + 1) * 128], HIO[k], HIO[k][:, c, :], IDN())
                    CP(H0, H0[:, :], pb, pb[0:64, 0:256])
                states.append({"S": H0, "i": 0})
            run_chains(lambda k, i, kk: ssd_body(grp[k][0], grp[k][1], (i if grp[k][1] == 0 else nt - 1 - i), k, kk, states[k]), len(grp), False)
            if ctx:
                for k, (s, d) in enumerate(grp):
                    Hc = states[k]["S"]
                    pb = bank()
                    for c in range(2):
                        TR(pb, pb[:, c * 64:(c + 1) * 64], Hc, Hc[:, c * 128:(c + 1) * 128], IDN(64))
                    CP(HIO[k], HIO[k][:, :, :], pb, pb[:, 0:128].rearrange("p (a b) -> p a b", a=2))
                    for c in range(2):
                        ST(o_ssd[s, l, d, c * 128:(c + 1) * 128, :], HIO[k], HIO[k][:, c, :], dw=dOUT)
        oi = 0
        for p0 in range(0, T, NF):
            n = NF
            for c in range(2):
                load_fm(ZT[c], ZT[c][:, 0:n], 15 + c, p0, n)
            for h in range(4):
                c, base = h // 2, 64 * (h % 2)
                ACT(DX, DX[:, 0:n], XC[c], XC[c][base:base + 64, p0:p0 + n], AF.Identity, scale=vf[base:base + 64, 17 + c:18 + c], rd=[vf])
                TT(YZ, YZ[:, h, 0:n], DX, DX[:, 0:n], YD, YD[:, h, p0:p0 + n], ALU.add)
                ACT(SZ, SZ[:, 0:n], ZT[c], ZT[c][base:base + 64, 0:n], AF.Silu)
                TT(YZ, YZ[:, h, 0:n], YZ, YZ[:, h, 0:n], SZ, SZ[:, 0:n], ALU.mult)
            ACT(SQd, SQd[:, :, 0:n], YZ, YZ[:, :, 0:n], AF.Square)
            pb = bank()
            for h in range(4):
                MM(pb, pb[0:64, 0:n], CST, CST[0:64, ONES, 0:64], SQd, SQd[:, h, 0:n], st=(h == 0), sp=(h == 3))
            rstd_from(RSd, RSd[:, 0:n], pb, pb[0:64, 0:n], 256.0)
            for h in range(4):
                O_ = OSd[oi % 2]
                oi += 1
                STT(O_, O_[:, 0:n], YZ, YZ[:, h, 0:n], vf[0:64, 19 + h:20 + h], RSd, RSd[:, 0:n], ALU.mult, ALU.mult, rd=[vf])
                ST(O_fm[768 + h * 64:768 + h * 64 + 64, p0:p0 + n], O_, O_[:, 0:n], dw=dO_fm)
        P.pop()
        if MAXPH < 6:
            return

        P.push()
        EPS_T = P.sb("EPS", [128, 1])
        MEMSET(EPS_T, EPS_T[:, :], EPS)
        WO = P.sb("WO", [128, 8, 1024])
        WOh = P.sb("WOh", [128, 8, 1024], F32R)
        WOl = P.sb("WOl", [128, 8, 1024], F32R)
        for k in range(8):
            LD(WO, WO[:, k, :], wout_d[l][:, k, :])
        for k in range(8):
            CP(WOh, WOh[:, k, :], WO, WO[:, k, :])
            TT(WOl, WOl[:, k, :], WO, WO[:, k, :], WOh, WOh[:, k, :].bitcast(F32), ALU.subtract)
        OT = [P.sb("OT%d" % i, [128, 8, 128]) for i in range(3)]
        OTh = [P.sb("OTh%d" % i, [128, 8, 128], F32R) for i in range(2)]
        OTl = [P.sb("OTl%d" % i, [128, 8, 128], F32R) for i in range(2)]
        XR = [P.sb("XR%d" % i, [128, 1024]) for i in range(3)]
        YT = [P.sb("YT%d" % i, [128, 1024]) for i in range(3)]
        SQo = P.sb("SQo", [128, 1024])
        S6 = [P.sb("S6_%d" % i, [128, 2]) for i in range(3)]
        def prep6(t):
            LD(OT[t % 3], OT[t % 3][:, :, :], O_fm[:, t * 128:(t + 1) * 128].rearrange("(k p) t -> p k t", p=128), dr=dO_fm)
            LD(XR[t % 3], XR[t % 3][:, :], xsrc[t * 128:(t + 1) * 128, :], dr=dxs)
            CP(OTh[t % 2], OTh[t % 2][:, :, :], OT[t % 3], OT[t % 3][:, :, :])
            TT(OTl[t % 2], OTl[t % 2][:, :, :], OT[t % 3], OT[t % 3][:, :, :], OTh[t % 2], OTh[t % 2][:, :, :].bitcast(F32), ALU.subtract)
        prep6(0)
        for t in range(NT):
            i2 = t % 3
            j2 = t % 2
            if t + 1 < NT:
                prep6(t + 1)
            pbs = [bank(), bank()]
            for hf in range(2):
                n = 0
                for k in range(8):
                    for (x, y) in ((OTh[j2], WOh), (OTh[j2], WOl), (OTl[j2], WOh)):
                        MM(pbs[hf], pbs[hf][:, :], x, x[:, k, :], y, y[:, k, hf * 512:(hf + 1) * 512], st=(n == 0), sp=(n == 23))
                        n += 1
                ACT(SQo, SQo[:, hf * 512:(hf + 1) * 512], pbs[hf], pbs[hf][:, :], AF.Square)
            s6 = S6[i2]
            RSUM(s6, s6[:, 0:1], SQo, SQo[:, :])
            rstd_from(s6, s6[:, 1:2], s6, s6[:, 0:1], 1024.0)
            Y = YT[i2]
            for hf in range(2):
                STT(Y, Y[:, hf * 512:(hf + 1) * 512], pbs[hf], pbs[hf][:, :], s6[:, 1:2], GBC[l][who], GBC[l][who][:, hf * 512:(hf + 1) * 512],
                    ALU.mult, ALU.mult, rd=[s6])
            TT(Y, Y[:, :], Y, Y[:, :], XR[i2], XR[i2][:, :], ALU.add)
            ST(ydst[t * 128:(t + 1) * 128, :], Y, Y[:, :], dw=dyd)
        P.pop()

    maxl = int(os.environ.get("MK_MAXL", "2"))
    jobs = [int(j) for j in os.environ.get("MK_JOBS", "01")]
    for job in jobs:
        for l in range(maxl):
            run_layer(job, l)
    P.emit()
    es.close()
    return nc


def _rope_tables():
    def tab(R, n_tok=2048):
        nf = R // 4
        rows = n_tok // 64
        row = np.repeat(np.arange(rows), 64).astype(np.float32)
        col = np.tile(np.arange(64), rows).astype(np.float32)
        inv = (10000.0 ** (-np.arange(nf, dtype=np.float32) / nf)).astype(np.float32)
        ar = row[None, :] * inv[:, None]
        ac = col[None, :] * inv[:, None]
        cos = np.concatenate([np.cos(ar), np.cos(ar), np.cos(ac), np.cos(ac)], 0)
        sin = np.concatenate([-np.sin(ar), np.sin(ar), -np.sin(ac), np.sin(ac)], 0)
        return cos.astype(np.float32), sin.astype(np.float32)
    c64, s64 = tab(64)
    r64 = np.stack([np.concatenate([c64, c64], 0), np.concatenate([s64, s64], 0)], 0)
    c32, s32 = tab(32)
    r32 = np.zeros((2, 128, 2048), np.float32)
    r32[0, 64:96] = c32
    r32[1, 64:96] = s32
    return np.ascontiguousarray(r64), r32


def _consts():
    c = np.zeros((128, 11, 128), np.float32)
    i = np.arange(128)
    same = (i[:, None] // 64) == (i[None, :] // 64)
    c[:, 0, :] = np.eye(128)
    c[:, 1, :] = 1.0
    c[:, 2, :] = same & (i[:, None] <= i[None, :])
    c[:, 3, :] = same & (i[:, None] >= i[None, :])
    c[:, 4, :] = same & (i[:, None] > i[None, :])
    c[:, 5, :] = same & (i[:, None] < i[None, :])
    c[0, 6, :] = 1.0
    c[1, 7, :] = 1.0
    c[:, 8, :] = same
    c[:, 9, :] = (i[:, None] // 32) == (i[None, :] // 32)
    sw = np.concatenate([_swap(64), 64 + _swap(64)])
    c[sw, 10, np.arange(128)] = 1.0
    return c


_NC_CACHE = {}


def kernel(**inp):
    f = lambda k: np.asarray(inp[k], np.float32)
    x_prompt, x_sample = f("x_prompt"), f("x_sample")
    w_in, w_out, w_ada, b_ada = f("w_in"), f("w_out"), f("w_ada"), f("b_ada")
    shared = {}
    shared["wada_fm"] = np.stack([np.stack([_kpc(w_ada[l][:, g * 128:(g + 1) * 128]) for g in range(16)]) for l in range(2)])
    shared["wada_g"] = np.stack([_kpc(w_ada[l][:, 2048:3072]) for l in range(2)])
    shared["bada_fm"] = np.stack([np.ascontiguousarray(b_ada[l][:2048].reshape(16, 128).T) for l in range(2)])
    shared["bada_g"] = np.stack([np.stack([b_ada[l][2048:], b_ada[l][2048:]]) for l in range(2)])
    shared["npre"] = np.stack([np.ascontiguousarray(f("norm_pre")[l].reshape(8, 128).T) for l in range(2)])
    shared["npost"] = np.stack([np.stack([f("norm_post")[l]] * 2) for l in range(2)])
    shared["wfm"] = np.stack([np.stack([_kpc(_take_cols(w_in[l], g)) for g in FMG]) for l in range(2)])
    shared["wtm0"] = np.stack([_kpc(_take_cols(w_in[l], TM0)) for l in range(2)])
    shared["wtm1"] = np.stack([_kpc(_take_cols(w_in[l], TM1)) for l in range(2)])
    shared["wout"] = np.stack([_kpc(w_out[l]) for l in range(2)])
    shared["cst"] = _consts()
    r64, r32 = _rope_tables()
    shared["rope64"], shared["rope32"] = r64, r32
    vecfm = np.zeros((2, 128, 24), np.float32)
    vectm = np.zeros((2, 128, 384), np.float32)
    cw = np.zeros((2, 128, 4, 5), np.float32)
    sw64 = _swap(64)
    for l in range(2):
        gq, gk = f("gqa_q_norm")[l], f("gqa_k_norm")[l]
        vecfm[l, :, 0] = np.tile(gq, 2)
        vecfm[l, :, 1] = np.tile(gq[sw64], 2)
        vecfm[l, :, 2] = np.tile(gk, 2)
        vecfm[l, :, 3] = np.tile(gk[sw64], 2)
        vecfm[l, 0:64, 4] = f("gla_norm")[l]
        vecfm[l, :, 5] = f("mla_q_norm")[l][0:128]
        vecfm[l, 0:64, 6] = f("mla_q_norm")[l][128:192]
        vecfm[l, :, 7] = f("mla_kv_norm")[l]
        cb = f("ssd_conv_b")[l]
        cwl = f("ssd_conv_w")[l]
        for g in range(4):
            vecfm[l, :, 8 + g] = cb[g * 128:(g + 1) * 128]
            cw[l, :, g, :] = cwl[:, g * 128:(g + 1) * 128].T
        sd = f("ssd_d")[l]
        vecfm[l, :, 17] = np.repeat(sd[0:2], 64)
        vecfm[l, :, 18] = np.repeat(sd[2:4], 64)
        sn = f("ssd_norm")[l]
        for h in range(4):
            vecfm[l, 0:64, 19 + h] = sn[h * 64:(h + 1) * 64]
        vectm[l, :, 0:128] = f("mla_kv_norm")[l][None, :]
        vectm[l, :, 128:192] = gk[None, :]
        vectm[l, :, 192:200] = f("ssd_dt_bias")[l].reshape(8)[None, :]
        vectm[l, :, 200:208] = f("ssd_a_log")[l].reshape(8)[None, :]
    shared["vecfm"], shared["vectm"], shared["cw"] = vecfm, vectm, cw
    wg = np.zeros((2, 2, 33, 128), np.float32)
    for l in range(2):
        for d in range(2):
            wg[l, d, 16 * d:16 * d + 16, :] = f("gla_w_gate")[l, d]
            wg[l, d, 32, :] = f("gla_b_gate")[l, d]
    shared["wg"] = wg
    wuq = np.zeros((2, 2, 128, 2, 384), np.float32)
    sw32 = _swap(32)
    for l in range(2):
        w = f("mla_w_uq")[l]
        ws = w.copy()
        for h in range(4):
            ws[:, h * 96 + 64:h * 96 + 96] = w[:, h * 96 + 64 + sw32]
        for v, ww in enumerate([w, ws]):
            wuq[l, v, :, 0, :] = ww[0:128]
            wuq[l, v, 0:64, 1, :] = ww[128:192]
    shared["wuq"] = wuq
    shared["wukv"] = f("mla_w_ukv")
    c_ctx, c = f("c_ctx"), f("c")
    in_maps = []
    for core in range(8):
        b = core // 4
        m = dict(shared)
        m["x_p"] = np.ascontiguousarray(x_prompt[4 * core:4 * core + 4].reshape(1024, 1024))
        m["x_s"] = np.ascontiguousarray(x_sample[b])
        cf = np.stack([c_ctx, c[b]], -1)
        m["cfm"] = np.ascontiguousarray(cf.reshape(8, 128, 2).transpose(1, 0, 2))
        m["c_gk"] = np.ascontiguousarray(f("cache_gqa_k")[b].reshape(2, 512, 128))
        m["c_gv"] = np.ascontiguousarray(f("cache_gqa_v")[b].reshape(2, 512, 128))
        m["c_ckv"] = np.ascontiguousarray(f("cache_mla_ckv")[b])
        m["c_kr"] = np.ascontiguousarray(f("cache_mla_krope")[b])
        m["s_gla"] = np.ascontiguousarray(f("state_gla")[b].reshape(2, 2, 128, 64))
        m["s_ssd"] = np.ascontiguousarray(f("state_ssd")[b].reshape(2, 2, 256, 64))
        in_maps.append(m)
    if "nc" not in _NC_CACHE:
        _NC_CACHE["nc"] = build_program()
    res = run_bass_kernel_spmd(_NC_CACHE["nc"], in_maps, core_ids=list(range(8)))
    R = res.results
    if DEBUG:
        _NC_CACHE["last"] = R
    y_prompt = np.concatenate([R[c_]["y_p"].reshape(4, 256, 1024) for c_ in range(8)], 0)
    y_sample = np.stack([R[0]["y_s"], R[4]["y_s"]], 0)
    cat = lambda k: np.concatenate([R[c_][k] for c_ in range(8)], 0)
    new_gqa_k = cat("o_gk").reshape(32, 2, 256, 2, 64)
    new_gqa_v = cat("o_gv").reshape(32, 2, 256, 2, 64)
    new_ckv = cat("o_ckv")
    new_kr = cat("o_kr")
    new_gla = cat("o_gla").reshape(32, 2, 2, 4, 32, 64)
    new_ssd = cat("o_ssd").reshape(32, 2, 2, 4, 64, 64)
    return (y_prompt.astype(np.float32), y_sample.astype(np.float32), new_gqa_k, new_gqa_v, new_ckv, new_kr, new_gla, new_ssd)
```
